# Optimizing a Trainium2 kernel written in Bass

```python
import math
import jax, jax.numpy as jnp
from jax import lax
import numpy as np

D_MODEL = 1024
BATCH = 8
SEQ = 4096
DEPTH = 4

N_MIXERS = 2
N_ATTN_LAYERS = (DEPTH + 1) // 2
N_SSM_LAYERS = DEPTH // 2

HEAD_DIM = 64
N_HEADS = D_MODEL // (2 * HEAD_DIM)
Q_BLOCK = 128

REL_BUCKETS = 32
REL_MAX_DIST = 128

GROUP_CH = 16
GROUPS = D_MODEL // GROUP_CH
SSM_STATE = 64
SSM_CHUNK = 128

D_FF = 2816
FFN_RESIDUAL = 0.5

PLE_DIM = 256

N_NORMS = 8
RMS_EPS = 1e-6
NEG_INF = -1e30

kernel_name = "hybrid_diffattn_s5_macaron_trunk"


def rmsnorm(x, g):
    xf = x.astype(jnp.float32)
    y = xf * lax.rsqrt(jnp.mean(xf * xf, axis=-1, keepdims=True) + RMS_EPS)
    return (y * g.astype(jnp.float32)).astype(x.dtype)


def swiglu(h, w_in, w_out):
    gu = h @ w_in
    return (jax.nn.silu(gu[..., :D_FF]) * gu[..., D_FF:]) @ w_out


def t5_bucket(n):
    n = jnp.maximum(n, 0)
    max_exact = REL_BUCKETS // 2
    nf = jnp.maximum(n, 1).astype(jnp.float32)
    large = max_exact + (jnp.log(nf / max_exact) / math.log(REL_MAX_DIST / max_exact)
                         * (REL_BUCKETS - max_exact)).astype(jnp.int32)
    large = jnp.minimum(large, REL_BUCKETS - 1)
    return jnp.where(n < max_exact, n, large)


def diff_attention(h, w_qkv, w_o, lam_vecs, subln_g, rel_bias, lambda_init):
    Bsz, S, _ = h.shape
    n_blk = S // Q_BLOCK
    qkv = h @ w_qkv
    q, k, v = jnp.split(qkv, 3, axis=-1)
    q = q.reshape(Bsz, S, N_HEADS, 2, HEAD_DIM).transpose(0, 2, 3, 1, 4) * (HEAD_DIM ** -0.5)
    k = k.reshape(Bsz, S, N_HEADS, 2, HEAD_DIM).transpose(0, 2, 3, 1, 4)
    v = v.reshape(Bsz, S, N_HEADS, 2 * HEAD_DIM).transpose(0, 2, 1, 3)

    lv = lam_vecs.astype(jnp.float32)
    lam = jnp.exp(jnp.sum(lv[0] * lv[1])) - jnp.exp(jnp.sum(lv[2] * lv[3])) + lambda_init

    q_blocks = jnp.moveaxis(q.reshape(Bsz, N_HEADS, 2, n_blk, Q_BLOCK, HEAD_DIM), 3, 0)
    k_pos = jnp.arange(S, dtype=jnp.int32)

    def block(args):
        q_blk, blk = args
        q_pos = blk * Q_BLOCK + jnp.arange(Q_BLOCK, dtype=jnp.int32)
        dist = q_pos[:, None] - k_pos[None, :]
        bias = rel_bias.astype(jnp.float32)[t5_bucket(dist)]
        bias = bias.reshape(Q_BLOCK, S, N_HEADS, 2).transpose(2, 3, 0, 1)
        s = jnp.einsum('bhmqd,bhmkd->bhmqk', q_blk, k).astype(jnp.float32) + bias
        s = jnp.where(dist >= 0, s, NEG_INF)
        prob = jax.nn.softmax(s, axis=-1)
        attn = prob[:, :, 0] - lam * prob[:, :, 1]
        return jnp.einsum('bhqk,bhkv->bhqv', attn.astype(v.dtype), v)

    o = lax.map(block, (q_blocks, jnp.arange(n_blk, dtype=jnp.int32)))
    o = jnp.moveaxis(o, 0, 2).reshape(Bsz, N_HEADS, S, 2 * HEAD_DIM)
    o = rmsnorm(o, subln_g) * (1.0 - lambda_init)
    o = o.transpose(0, 2, 1, 3).reshape(Bsz, S, D_MODEL)
    return o @ w_o


def s5_glu(h, lam_re, lam_im, log_dt, b_re, b_im, c_re, c_im, d_skip, w_glu, b_glu):
    Bsz, S, _ = h.shape
    n_chunks = S // SSM_CHUNK
    lam = lax.complex(lam_re.astype(jnp.float32), lam_im.astype(jnp.float32))
    dt = jnp.exp(log_dt.astype(jnp.float32))[:, None]
    lam_dt = lam * dt
    a_bar = jnp.exp(lam_dt)
    b = lax.complex(b_re.astype(jnp.float32), b_im.astype(jnp.float32))
    b_bar = ((a_bar - 1.0) / lam)[:, :, None] * b
    c = lax.complex(c_re.astype(jnp.float32), c_im.astype(jnp.float32))
    d_g = d_skip.astype(jnp.float32).reshape(GROUPS, GROUP_CH)
    steps = jnp.arange(1, SSM_CHUNK + 1, dtype=jnp.float32)
    a_pow = jnp.exp(steps[:, None, None] * lam_dt[None])

    u = h.astype(jnp.float32).reshape(Bsz, n_chunks, SSM_CHUNK, GROUPS, GROUP_CH)
    u = jnp.moveaxis(u, 1, 0)

    def combine(e1, e2):
        a1, x1 = e1
        a2, x2 = e2
        return a1 * a2, a2 * x1 + x2

    def chunk_step(state, u_c):
        bu = jnp.einsum('gph,bcgh->bcgp', b_bar, u_c.astype(jnp.complex64))
        a = jnp.broadcast_to(a_bar, bu.shape)
        _, xs = lax.associative_scan(combine, (a, bu), axis=1)
        xs = xs + a_pow[None] * state[:, None]
        y = jnp.einsum('ghp,bcgp->bcgh', c, xs).real + d_g * u_c
        return xs[:, -1], y

    state0 = jnp.zeros((Bsz, GROUPS, SSM_STATE), jnp.complex64)
    _, ys = lax.scan(chunk_step, state0, u)
    y = jnp.moveaxis(ys, 0, 1).reshape(Bsz, S, D_MODEL).astype(h.dtype)
    z = jax.nn.gelu(y) @ w_glu + b_glu
    return z[..., :D_MODEL] * jax.nn.sigmoid(z[..., D_MODEL:])


def setup_inputs(seed: int = 0) -> dict:
    key = jax.random.key(seed)
    ks = jax.random.split(key, 24)
    f32 = jnp.float32
    nrm = lambda k, shape, scale: jax.random.normal(k, shape, f32) * scale

    x = nrm(ks[0], (BATCH, SEQ, D_MODEL), 1.0)
    p = nrm(ks[1], (DEPTH, BATCH, SEQ, PLE_DIM), 1.0)
    norm_g = 1.0 + nrm(ks[2], (DEPTH, N_NORMS, D_MODEL), 0.05)
    ffn_w_in = nrm(ks[3], (DEPTH, 2, D_MODEL, 2 * D_FF), D_MODEL ** -0.5)
    ffn_w_out = nrm(ks[4], (DEPTH, 2, D_FF, D_MODEL), D_FF ** -0.5)

    attn_w_qkv = nrm(ks[5], (N_ATTN_LAYERS, D_MODEL, 3 * D_MODEL), D_MODEL ** -0.5)
    attn_w_o = nrm(ks[6], (N_ATTN_LAYERS, D_MODEL, D_MODEL), D_MODEL ** -0.5)
    attn_lam = nrm(ks[7], (N_ATTN_LAYERS, 4, HEAD_DIM), 0.1)
    attn_subln_g = 1.0 + nrm(ks[8], (N_ATTN_LAYERS, 2 * HEAD_DIM), 0.05)
    rel_bias = nrm(ks[9], (REL_BUCKETS, 2 * N_HEADS), 0.5)

    n_idx = jnp.arange(SSM_STATE, dtype=f32)
    ssm_lam_re = jnp.full((N_SSM_LAYERS, GROUPS, SSM_STATE), -0.5, f32) + nrm(ks[10], (N_SSM_LAYERS, GROUPS, SSM_STATE), 1e-3)
    ssm_lam_im = jnp.broadcast_to(math.pi * n_idx, (N_SSM_LAYERS, GROUPS, SSM_STATE)) + nrm(ks[11], (N_SSM_LAYERS, GROUPS, SSM_STATE), 1e-3)
    ssm_log_dt = jax.random.uniform(ks[12], (N_SSM_LAYERS, GROUPS), f32, math.log(1e-3), math.log(1e-1))
    b_scale = (2.0 * GROUP_CH) ** -0.5
    c_scale = (2.0 * SSM_STATE) ** -0.5
    ssm_b_re = nrm(ks[13], (N_SSM_LAYERS, GROUPS, SSM_STATE, GROUP_CH), b_scale)
    ssm_b_im = nrm(ks[14], (N_SSM_LAYERS, GROUPS, SSM_STATE, GROUP_CH), b_scale)
    ssm_c_re = nrm(ks[15], (N_SSM_LAYERS, GROUPS, GROUP_CH, SSM_STATE), c_scale)
    ssm_c_im = nrm(ks[16], (N_SSM_LAYERS, GROUPS, GROUP_CH, SSM_STATE), c_scale)
    ssm_d = nrm(ks[17], (N_SSM_LAYERS, D_MODEL), 1.0)
    ssm_w_glu = nrm(ks[18], (N_SSM_LAYERS, D_MODEL, 2 * D_MODEL), D_MODEL ** -0.5)
    ssm_b_glu = nrm(ks[19], (N_SSM_LAYERS, 2 * D_MODEL), 0.01)

    ple_w_proj = nrm(ks[20], (DEPTH, PLE_DIM, D_MODEL), PLE_DIM ** -0.5)
    ple_w_gate = nrm(ks[21], (DEPTH, D_MODEL, D_MODEL), D_MODEL ** -0.5)

    return {"x": x, "p": p, "norm_g": norm_g, "ffn_w_in": ffn_w_in, "ffn_w_out": ffn_w_out,
            "attn_w_qkv": attn_w_qkv, "attn_w_o": attn_w_o, "attn_lam": attn_lam,
            "attn_subln_g": attn_subln_g, "rel_bias": rel_bias,
            "ssm_lam_re": ssm_lam_re, "ssm_lam_im": ssm_lam_im, "ssm_log_dt": ssm_log_dt,
            "ssm_b_re": ssm_b_re, "ssm_b_im": ssm_b_im, "ssm_c_re": ssm_c_re, "ssm_c_im": ssm_c_im,
            "ssm_d": ssm_d, "ssm_w_glu": ssm_w_glu, "ssm_b_glu": ssm_b_glu,
            "ple_w_proj": ple_w_proj, "ple_w_gate": ple_w_gate}


def reference(x, p, norm_g, ffn_w_in, ffn_w_out, attn_w_qkv, attn_w_o, attn_lam, attn_subln_g,
              rel_bias, ssm_lam_re, ssm_lam_im, ssm_log_dt, ssm_b_re, ssm_b_im, ssm_c_re, ssm_c_im,
              ssm_d, ssm_w_glu, ssm_b_glu, ple_w_proj, ple_w_gate):
    for i in range(DEPTH):
        g = norm_g[i]
        x = x + FFN_RESIDUAL * rmsnorm(swiglu(rmsnorm(x, g[0]), ffn_w_in[i, 0], ffn_w_out[i, 0]), g[1])
        h = rmsnorm(x, g[2])
        j = i // N_MIXERS
        if i % N_MIXERS == 0:
            lambda_init = 0.8 - 0.6 * math.exp(-0.3 * i)
            m = diff_attention(h, attn_w_qkv[j], attn_w_o[j], attn_lam[j], attn_subln_g[j],
                               rel_bias, lambda_init)
        else:
            m = s5_glu(h, ssm_lam_re[j], ssm_lam_im[j], ssm_log_dt[j], ssm_b_re[j], ssm_b_im[j],
                       ssm_c_re[j], ssm_c_im[j], ssm_d[j], ssm_w_glu[j], ssm_b_glu[j])
        x = x + rmsnorm(m, g[3])
        x = x + FFN_RESIDUAL * rmsnorm(swiglu(rmsnorm(x, g[4]), ffn_w_in[i, 1], ffn_w_out[i, 1]), g[5])
        gate = jax.nn.sigmoid(rmsnorm(x, g[6]) @ ple_w_gate[i])
        x = x + rmsnorm(gate * (p[i] @ ple_w_proj[i]), g[7])
    return x
```

```python
from contextlib import ExitStack
import os
import numpy as np
import concourse.bass as bass
import concourse.mybir as mybir
from concourse.bass_utils import run_bass_kernel_spmd

F32 = mybir.dt.float32
BF16 = mybir.dt.bfloat16
AF = mybir.ActivationFunctionType
ALU = mybir.AluOpType
AX = mybir.AxisListType

D = 1024
S = 4096
DFF = 2816
NFC = DFF // 128
DEPTH = 4
PLE = 256
EPS = 1e-6
NCORES = 8
DBG_NB = int(os.environ.get('KDBG_NB', '0'))
ATTACH_WAIT = int(os.environ.get('KATTACH', '1'))


class Res:
    __slots__ = ("lastw", "readers")

    def __init__(self):
        self.lastw = None
        self.readers = {}


class EngQ:
    def __init__(self, name, sem):
        self.name = name
        self.sem = sem
        self.count = 0
        self.waited = {}
        self.ops = []


class KB:
    def __init__(self):
        self.nc = bass.Bass("TRN2", target_bir_lowering=False)
        nc = self.nc
        self.q = {}
        for n in ("pe", "act", "dve", "pool", "sp"):
            self.q[n] = EngQ(n, nc.alloc_semaphore("prog_" + n))
        self.res = {}
        self.dsem = {}
        self.dcount = {}
        self.free_sems = {}
        self.sem_eng = {}
        self.all_sems = []
        self.stc = 0

    def _res(self, key):
        r = self.res.get(key)
        if r is None:
            r = self.res[key] = Res()
        return r

    def _collect(self, q, r, w):
        need = {}

        def add(tok, same_ok):
            if tok is None:
                return
            sem, val = tok
            if sem is q.sem and not same_ok:
                return
            if need.get(sem, 0) < val:
                need[sem] = val

        raw_same = q.name in ("act", "dve", "pool")
        for key in r:
            add(self._res(key).lastw, raw_same)
        for key in w:
            rs = self._res(key)
            add(rs.lastw, False)
            for sem, val in rs.readers.items():
                add((sem, val), raw_same)
        waits = []
        for sem, val in need.items():
            if q.waited.get(sem, 0) >= val:
                continue
            q.waited[sem] = val
            waits.append((sem, val))
        return waits

    def _update(self, tok, r, w):
        sem, val = tok
        for key in r:
            rs = self._res(key)
            if rs.readers.get(sem, 0) < val:
                rs.readers[sem] = val
        for key in w:
            rs = self._res(key)
            rs.lastw = tok
            rs.readers = {}

    def op(self, eng, fn, r=(), w=()):
        q = self.q[eng]
        waits = self._collect(q, r, w)
        q.count += 1
        tok = (q.sem, q.count)
        q.ops.append((waits, fn, (q.sem, 1)))
        self._update(tok, r, w)
        return tok

    def dma(self, eng, out, in_, r=(), w=(), sem=None, slow=False):
        q = self.q[eng]
        if sem not in self.dsem:
            fl = self.free_sems.setdefault(eng, [])
            if fl:
                self.dsem[sem] = fl.pop()
            else:
                sh = self.nc.alloc_semaphore("dma_%d" % len(self.all_sems))
                self.all_sems.append(sh)
                self.sem_eng[id(sh)] = eng
                self.dcount[id(sh)] = 0
                self.dsem[sem] = sh
        s = self.dsem[sem]
        waits = self._collect(q, r, w)
        self.dcount[id(s)] += 16
        tok = (s, self.dcount[id(s)])
        if slow:
            fn = lambda e, out=out, in_=in_: e.dma_start(out=out, in_=in_, allow_slow_non_contiguous=True)
        else:
            fn = lambda e, out=out, in_=in_: e.dma_start(out=out, in_=in_)
        q.ops.append((waits, fn, (s, 16)))
        self._update(tok, r, w)
        return tok

    def wait_all(self, eng, keys):
        q = self.q[eng]
        waits = self._collect(q, (), keys)
        q.ops.append((waits, None, None))

    def st(self):
        self.stc = (self.stc + 1) % 96
        return self.stc

    def barrier(self):
        toks = [(q.sem, q.count) for q in self.q.values() if q.count > 0]
        toks += [(sh, self.dcount[id(sh)]) for sh in self.all_sems if self.dcount[id(sh)] > 0]
        for q in self.q.values():
            waits = []
            for sem, val in toks:
                if sem is q.sem or q.waited.get(sem, 0) >= val:
                    continue
                q.waited[sem] = val
                waits.append((sem, val))
            q.ops.append((waits, None, None))
        self.free_sems = {}
        for sh in self.all_sems:
            self.free_sems.setdefault(self.sem_eng[id(sh)], []).append(sh)
        self.dsem = {}

    def emit(self):
        nc = self.nc
        with nc.Block() as block:
            def mk(qn):
                q = self.q[qn]

                def body(e):
                    for waits, fn, inc in q.ops:
                        attach = None
                        if ATTACH_WAIT and fn is not None and waits:
                            attach = waits[-1]
                            waits = waits[:-1]
                        for sem, val in waits:
                            e.wait_ge(sem, val)
                        if fn is not None:
                            ins = fn(e)
                            if attach is not None:
                                ins._wait_ge(attach[0], attach[1])
                            ins.then_inc(inc[0], inc[1])
                return body
            block.tensor(mk("pe"))
            block.scalar(mk("act"))
            block.vector(mk("dve"))
            block.gpsimd(mk("pool"))
            block.sync(mk("sp"))


class Prog:
    def __init__(self, phases):
        self.kb = KB()
        self.nc = self.kb.nc
        self.phases = phases
        self.es = ExitStack()
        self._held = None

    def sb(self, name, shape, dt):
        return self.es.enter_context(self.nc.sbuf_tensor(name, shape, dt))

    def ps(self, name, shape, dt=F32):
        return self.es.enter_context(self.nc.psum_tensor(name, shape, dt))

    def build(self):
        nc = self.nc
        kb = self.kb
        di = lambda n, sh: nc.dram_tensor(n, sh, F32, kind="ExternalInput").ap()
        self.x_in = di("x", [S, D])
        self.p_in = di("p", [DEPTH, S, PLE])
        self.norm_g = di("norm_g", [DEPTH, 8, D])
        self.ffn_w_in = di("ffn_w_in", [DEPTH, 2, D, 2 * DFF])
        self.ffn_w_out = di("ffn_w_out", [DEPTH, 2, DFF, D])
        self.ple_w_proj = di("ple_w_proj", [DEPTH, PLE, D])
        self.ple_w_gate = di("ple_w_gate", [DEPTH, D, D])
        self.ident_in = di("ident_in", [128, 128])
        self.jflip_in = di("jflip_in", [128, 128])
        self.oh_in = di("oh_in", [32, 128])
        self.attn_w_qkv = di("attn_w_qkv", [2, D, 3 * D])
        self.attn_w_o = di("attn_w_o", [2, D, D])
        self.attn_lam = di("attn_lam", [2, 4, 64])
        self.attn_subln_g = di("attn_subln_g", [2, 128])
        self.rel_bias = di("rel_bias", [32, 16])
        self.mask_in = di("mask_in", [128, 128])
        for nm, sh in (("ssm_lam_re", [2, 64, 64]), ("ssm_lam_im", [2, 64, 64]), ("ssm_log_dt", [2, 64]),
                       ("ssm_b_re", [2, 64, 64, 16]), ("ssm_b_im", [2, 64, 64, 16]), ("ssm_c_re", [2, 64, 16, 64]),
                       ("ssm_c_im", [2, 64, 16, 64]), ("ssm_d", [2, 1024]), ("ssm_w_glu", [2, D, 2 * D]),
                       ("ssm_b_glu", [2, 2 * D])):
            setattr(self, nm, di(nm, sh))
        self.yd = nc.dram_tensor("yd_scr", [8, 128, 8, 512], BF16).ap()
        self.hd = nc.dram_tensor("hd_scr", [8, 128, 8, 512], BF16).ap()
        self.od_h = nc.dram_tensor("od_scr", [S, D], F32)
        self.od = self.od_h.ap()
        self.wscr_h = nc.dram_tensor("w_scr", [16, 384], F32)
        self.wscr = self.wscr_h.ap()
        self.xd = nc.dram_tensor("out", [S, D], F32, kind="ExternalOutput").ap()

        with self.es:
            self.ident = self.sb("ident", [128, 128], BF16)
            kb.dma("pool", self.ident[:], self.ident_in[:, :], w=["ident"], sem="const")
            self.gT = self.sb("gT", [128, DEPTH * 8, 8], F32)
            nc_g = self.norm_g.rearrange("l n (kc p) -> p (l n) kc", p=128)
            kb.q["sp"].ops.append(([], None, None))
            kb.dma("sp", self.gT[:], nc_g, w=["gT"], sem="const2", slow=True)
            self.stt = self.sb("stats", [128, 96], F32)
            self.epsb = self.sb("epsb", [128, 1], F32)
            kb.op("dve", lambda e: e.memset(self.epsb[:], EPS), w=["epsb"])
            self.junk = self.sb("junk", [128, 1024], BF16)
            self.jflip = self.sb("jflip", [128, 128], BF16)
            kb.dma("pool", self.jflip[:], self.jflip_in[:, :], w=["jflip"], sem="ac3")

            first = True
            PREF = int(os.environ.get("KPREF", "1"))
            phs = self.phases
            pending = None
            for pi, ph in enumerate(phs):
                src = self.x_in if first else self.xd
                kind = ph[0]
                nxt = phs[pi + 1] if pi + 1 < len(phs) else None
                with ExitStack() as wes:
                    def prefetch_next():
                        if PREF and nxt is not None and nxt[0] == "ffn":
                            win, wout, start_w = self.ffn_weights(wes, nxt[1], nxt[2])
                            return (win, wout), start_w
                        return None, None
                    if kind == "ffn":
                        self.phase_ffn(ph[1], ph[2], src, pre=pending)
                        pending = None
                        if self._held is not None:
                            self._held.close()
                            self._held = None
                    elif kind == "ple":
                        pw, start_w = prefetch_next()
                        self.phase_ple(ph[1], src, start_w)
                        if pw is not None:
                            pending = pw
                            self._held = wes.pop_all()
                    elif kind == "attn":
                        assert not first
                        self.phase_attn(ph[1], src)
                        pw, start_w = prefetch_next()
                        self.phase_attn_out(ph[1], start_w)
                        if pw is not None:
                            pending = pw
                            self._held = wes.pop_all()
                    elif kind == "ssm":
                        assert not first
                        self.phase_ssm(ph[1], src)
                        self.phase_glu(ph[1])
                    elif kind == "copy":
                        for b in range(16):
                            kb.dma("sp", self.xd[b * 256:(b + 1) * 256, :], self.x_in[b * 256:(b + 1) * 256, :],
                                   w=[("xd", b)], sem="cp%d" % (b % 4))
                        kb.barrier()
                    else:
                        raise ValueError(kind)
                first = False
            keys = [("xd", b) for b in range(64)]
            kb.wait_all("sp", keys)
            kb.emit()
        return nc

    def rstd(self, src_ap, src_key, n_elem, extra_r=()):
        kb = self.kb
        st = self.stt
        c1, c2, c3 = kb.st(), kb.st(), kb.st()
        kb.op("act", lambda e: e.activation(out=self.junk[:, 0:n_elem], in_=src_ap, func=AF.Square,
                                            accum_out=st[:, c1:c1 + 1]),
              r=[src_key, *extra_r], w=["junk", ("st", c1)])
        kb.op("act", lambda e: e.activation(out=st[:, c2:c2 + 1], in_=st[:, c1:c1 + 1], func=AF.Sqrt,
                                            scale=1.0 / n_elem, bias=self.epsb[:, 0:1]),
              r=[("st", c1), "epsb"], w=[("st", c2)])
        kb.op("dve", lambda e: e.reciprocal(out=st[:, c3:c3 + 1], in_=st[:, c2:c2 + 1]),
              r=[("st", c2)], w=[("st", c3)])
        return c3

    def load_w(self, dst, dst_key, src_ap, nsplit, axis_len, sem):
        kb = self.kb
        step = axis_len // nsplit
        for i in range(nsplit):
            c0, c1 = i * step, (i + 1) * step
            kb.dma("pool", dst[:, :, c0:c1], src_ap[:, :, c0:c1], w=[dst_key], sem=sem)

    def epilogue(self, m_ap, m_key, xt_ap, xt_key, gpost, coef, tmp):
        kb = self.kb
        c = self.rstd(m_ap, m_key, D)
        st = self.stt
        kb.op("dve", lambda e: e.scalar_tensor_tensor(out=tmp[:], in0=m_ap, scalar=st[:, c:c + 1], in1=gpost[:],
                                                      op0=ALU.mult, op1=ALU.mult),
              r=[m_key, ("st", c), "gpost"], w=["tmp"])
        kb.op("dve", lambda e: e.scalar_tensor_tensor(out=xt_ap, in0=tmp[:], scalar=float(coef), in1=xt_ap,
                                                      op0=ALU.mult, op1=ALU.add),
              r=["tmp", xt_key], w=[xt_key])

    def ffn_weights(self, es, layer, which):
        nc = self.nc
        tag = "f%d%d" % (layer, which)
        win = es.enter_context(nc.sbuf_tensor(tag + "win", [128, 8, 2 * DFF], BF16))
        wout = es.enter_context(nc.sbuf_tensor(tag + "wout", [128, NFC, D], BF16))

        def start():
            self.load_w(win, tag + "win", self.ffn_w_in[layer, which].rearrange("(kc p) n -> p kc n", p=128),
                        8, 2 * DFF, tag + "win")
            self.load_w(wout, tag + "wout", self.ffn_w_out[layer, which].rearrange("(fc p) n -> p fc n", p=128),
                        4, D, tag + "wout")
        return (win, wout, start)

    def phase_ffn(self, layer, which, src, pre=None):
        kb = self.kb
        nc = self.nc
        TB = 256
        NB = DBG_NB or (S // TB)
        gi_pre = layer * 8 + (0 if which == 0 else 4)
        gi_post = (1 if which == 0 else 5)
        tag = "f%d%d" % (layer, which)
        with ExitStack() as es:
            sb = lambda n, sh, dt: es.enter_context(nc.sbuf_tensor(tag + n, sh, dt))
            ps = lambda n, sh, dt=F32: es.enter_context(nc.psum_tensor(tag + n, sh, dt))
            if pre is None:
                win, wout, start_w = self.ffn_weights(es, layer, which)
                start_w()
            else:
                win, wout = pre
            gpost = sb("gpost", [128, D], F32)
            xt = [sb("xt%d" % i, [128, 2, D], F32) for i in range(2)]
            hb = sb("hb", [128, 2, D], BF16)
            hT = sb("hT", [128, 8, TB], BF16)
            aT = sb("aT", [128, NFC, TB], BF16)
            sg = [sb("sg%d" % i, [128, TB], BF16) for i in range(2)]
            tmp = sb("tmp", [128, D], F32)
            tp = [ps("tp%d" % i, [128, 4, TB], BF16) for i in range(2)]
            gu = [ps("gu%d" % i, [128, 2, TB]) for i in range(2)]
            ob = [ps("ob%d" % i, [128, 2, 512]) for i in range(2)]
            K = lambda n: tag + n

            kb.dma("sp", gpost[:], self.norm_g[layer, gi_post, :].partition_broadcast(128), w=["gpost"], sem=K("gp"))

            def load(b):
                s = b % 2
                kb.dma("sp", xt[s][:], src[b * TB:(b + 1) * TB, :].rearrange("(j p) d -> p j d", p=128),
                       r=[("xd", b)], w=[K("xt%d" % s)], sem=K("xt%d" % s))

            def A1(b):
                s = b % 2
                for j in range(2):
                    c = self.rstd(xt[s][:, j, :], K("xt%d" % s), D)
                    kb.op("act", lambda e, j=j, c=c, s=s: e.activation(
                        out=hb[:, j, :], in_=xt[s][:, j, :], func=AF.Copy, scale=self.stt[:, c:c + 1]),
                        r=[K("xt%d" % s), ("st", c)], w=[K("hb")])

            def A2(b):
                for g in range(2):
                    t = tp[g]
                    for kci in range(4):
                        kc = g * 4 + kci
                        for j in range(2):
                            kb.op("pe", lambda e, t=t, kci=kci, kc=kc, j=j: e.transpose(
                                out=t[:, kci, j * 128:(j + 1) * 128], in_=hb[:, j, kc * 128:(kc + 1) * 128],
                                identity=self.ident[:]), r=[K("hb"), "ident"], w=[K("tp%d" % g)])
                    gsl = self.gT[:, gi_pre, g * 4:(g + 1) * 4].unsqueeze(2).to_broadcast([128, 4, TB])
                    kb.op("dve", lambda e, t=t, g=g, gsl=gsl: e.tensor_tensor(
                        out=hT[:, g * 4:(g + 1) * 4, :], in0=t[:], in1=gsl, op=ALU.mult),
                        r=[K("tp%d" % g), "gT"], w=[K("hT")])

            def B(b):
                for fc in range(NFC):
                    bank = gu[fc % 2]
                    bk = K("gu%d" % (fc % 2))
                    for half in range(2):
                        c0 = half * DFF + fc * 128
                        for kc in range(8):
                            kb.op("pe", lambda e, bank=bank, half=half, c0=c0, kc=kc: e.matmul(
                                bank[:, half, :], lhsT=win[:, kc, c0:c0 + 128], rhs=hT[:, kc, :],
                                start=(kc == 0), stop=(kc == 7)), r=[K("win"), K("hT")], w=[bk])
                    sgt = sg[fc % 2]
                    sk = K("sg%d" % (fc % 2))
                    kb.op("act", lambda e, bank=bank, sgt=sgt: e.activation(out=sgt[:], in_=bank[:, 0, :], func=AF.Silu),
                          r=[bk], w=[sk])
                    kb.op("dve", lambda e, bank=bank, sgt=sgt, fc=fc: e.tensor_tensor(
                        out=aT[:, fc, :], in0=bank[:, 1, :], in1=sgt[:], op=ALU.mult),
                        r=[bk, sk], w=[K("aT")])

            def C(b):
                s = b % 2
                for j in range(2):
                    o = ob[j]
                    ok = K("ob%d" % j)
                    for half in range(2):
                        for fc in range(NFC):
                            kb.op("pe", lambda e, o=o, half=half, fc=fc, j=j: e.matmul(
                                o[:, half, :], lhsT=aT[:, fc, j * 128:(j + 1) * 128],
                                rhs=wout[:, fc, half * 512:(half + 1) * 512],
                                start=(fc == 0), stop=(fc == NFC - 1)), r=[K("aT"), K("wout")], w=[ok])
                    self.epilogue(o[:].rearrange("p a b -> p (a b)"), ok, xt[s][:, j, :], K("xt%d" % s), gpost, 0.5, tmp)
                kb.dma("sp", self.xd[b * TB:(b + 1) * TB, :].rearrange("(j p) d -> p j d", p=128), xt[s][:],
                       r=[K("xt%d" % s)], w=[("xd", b)], sem=K("xt%d" % s))

            load(0)
            A1(0)
            A2(0)
            for b in range(NB):
                if b + 1 < NB:
                    load(b + 1)
                B(b)
                if b + 1 < NB:
                    A1(b + 1)
                C(b)
                if b + 1 < NB:
                    A2(b + 1)
            kb.barrier()

    def prenorm_hb(self, xt_ap, xt_key, hb_ap, hb_key):
        kb = self.kb
        c = self.rstd(xt_ap, xt_key, D)
        kb.op("act", lambda e: e.activation(out=hb_ap, in_=xt_ap, func=AF.Copy, scale=self.stt[:, c:c + 1]),
              r=[xt_key, ("st", c)], w=[hb_key])

    def transposes(self, hb, hb_key, nj, nkc, hT, hT_key, col0, tp, tp_keys, gain_ap_fn, gain_key, layout8=False):
        kb = self.kb
        W = nj * 128
        for g in range((nkc + 3) // 4):
            t = tp[g % 2]
            tk = tp_keys[g % 2]
            n4 = min(4, nkc - g * 4)
            for kci in range(n4):
                kc = g * 4 + kci
                for j in range(nj):
                    kb.op("pe", lambda e, t=t, kci=kci, kc=kc, j=j: e.transpose(
                        out=t[:, kci, j * 128:(j + 1) * 128], in_=hb[:, j, kc * 128:(kc + 1) * 128],
                        identity=self.ident[:]), r=[hb_key, "ident"], w=[tk])
            if layout8:
                dst8 = hT[:, g * 4:g * 4 + n4, :, col0 // 8:(col0 + W) // 8]
                src8 = t[:, 0:n4, 0:W].rearrange("p k (b s) -> p k s b", s=8)
                gsl8 = gain_ap_fn(g * 4, n4).unsqueeze(2).unsqueeze(3).to_broadcast([128, n4, 8, W // 8])
                kb.op("dve", lambda e, dst8=dst8, src8=src8, gsl8=gsl8: e.tensor_tensor(
                    out=dst8, in0=src8, in1=gsl8, op=ALU.mult), r=[tk, gain_key], w=[hT_key])
                continue
            dst = hT[:, g * 4:g * 4 + n4, col0:col0 + W]
            if gain_ap_fn is None:
                kb.op("act", lambda e, t=t, dst=dst, n4=n4: e.copy(out=dst, in_=t[:, 0:n4, 0:W]), r=[tk], w=[hT_key])
            else:
                gsl = gain_ap_fn(g * 4, n4).unsqueeze(2).to_broadcast([128, n4, W])
                kb.op("dve", lambda e, t=t, dst=dst, gsl=gsl, n4=n4: e.tensor_tensor(
                    out=dst, in0=t[:, 0:n4, 0:W], in1=gsl, op=ALU.mult), r=[tk, gain_key], w=[hT_key])

    def phase_ple(self, layer, src, start_w=None):
        kb = self.kb
        nc = self.nc
        TB = 256 if start_w is not None else 512
        NJ = TB // 128
        NB = DBG_NB or (S // TB)
        PIPE = int(os.environ.get('KPLE_PIPE', '0'))
        gi_pre = layer * 8 + 6
        tag = "p%d" % layer
        with ExitStack() as es:
            sb = lambda n, sh, dt: es.enter_context(nc.sbuf_tensor(tag + n, sh, dt))
            ps = lambda n, sh, dt=F32: es.enter_context(nc.psum_tensor(tag + n, sh, dt))
            wg = sb("wg", [128, 8, D], BF16)
            wp = sb("wp", [128, 2, D], BF16)
            gpost = sb("gpost", [128, D], F32)
            xt = [sb("xt%d" % i, [128, NJ, D], F32) for i in range(2)]
            pt = [sb("pt%d" % i, [128, NJ, PLE], F32) for i in range(2)]
            hbs = [sb("hb%d" % i, [128, NJ, D], BF16) for i in range(1 + PIPE)]
            pbs = [sb("pb%d" % i, [128, NJ, PLE], BF16) for i in range(1 + PIPE)]
            hTs = [sb("hT%d" % i, [128, 8, TB], BF16) for i in range(1 + PIPE)]
            pTs = [sb("pT%d" % i, [128, 2, TB], BF16) for i in range(1 + PIPE)]
            sgm = sb("sgm", [128, D], F32)
            vt = sb("vt", [128, D], F32)
            tmp = sb("tmp", [128, D], F32)
            tp = [ps("tp%d" % i, [128, 4, TB], BF16) for i in range(2)]
            gate = ps("gate", [128, 2, 512])
            pe_ = ps("pe", [128, 2, 512])
            K = lambda n: tag + n
            self.load_w(wg, K("wg"), self.ple_w_gate[layer].rearrange("(kc p) n -> p kc n", p=128), 2, D, K("wg"))
            self.load_w(wp, K("wp"), self.ple_w_proj[layer].rearrange("(kc p) n -> p kc n", p=128), 1, D, K("wp"))
            kb.dma("sp", gpost[:], self.norm_g[layer, 7, :].partition_broadcast(128), w=["gpost"], sem=K("gp"))
            if start_w is not None:
                start_w()

            def load(b):
                s = b % 2
                kb.dma("sp", xt[s][:], src[b * TB:(b + 1) * TB, :].rearrange("(j p) d -> p j d", p=128),
                       r=[("xd", b)], w=[K("xt%d" % s)], sem=K("xt%d" % s))
                kb.dma("sp", pt[s][:], self.p_in[layer, b * TB:(b + 1) * TB, :].rearrange("(j p) d -> p j d", p=128),
                       w=[K("pt%d" % s)], sem=K("pt%d" % s))

            def A(b):
                s = b % 2
                s2 = s if PIPE else 0
                hb, pb, hT, pT = hbs[s2], pbs[s2], hTs[s2], pTs[s2]
                for j in range(NJ):
                    self.prenorm_hb(xt[s][:, j, :], K("xt%d" % s), hb[:, j, :], K("hb%d" % s2))
                kb.op("dve", lambda e, s=s, pb=pb: e.tensor_copy(out=pb[:], in_=pt[s][:]), r=[K("pt%d" % s)], w=[K("pb%d" % s2)])
                self.transposes(hb, K("hb%d" % s2), NJ, 8, hT, K("hT%d" % s2), 0, tp, [K("tp0"), K("tp1")],
                                lambda k0, n: self.gT[:, gi_pre, k0:k0 + n], "gT")
                self.transposes(pb, K("pb%d" % s2), NJ, 2, pT, K("pT%d" % s2), 0, tp, [K("tp0"), K("tp1")], None, None)

            def C(b):
                s = b % 2
                s2 = s if PIPE else 0
                hT, pT = hTs[s2], pTs[s2]
                for j in range(NJ):
                    for half in range(2):
                        for kc in range(8):
                            kb.op("pe", lambda e, half=half, kc=kc, j=j, hT=hT: e.matmul(
                                gate[:, half, :], lhsT=hT[:, kc, j * 128:(j + 1) * 128],
                                rhs=wg[:, kc, half * 512:(half + 1) * 512], start=(kc == 0), stop=(kc == 7)),
                                r=[K("hT%d" % s2), K("wg")], w=[K("gate")])
                    for half in range(2):
                        for kc in range(2):
                            kb.op("pe", lambda e, half=half, kc=kc, j=j, pT=pT: e.matmul(
                                pe_[:, half, :], lhsT=pT[:, kc, j * 128:(j + 1) * 128],
                                rhs=wp[:, kc, half * 512:(half + 1) * 512], start=(kc == 0), stop=(kc == 1)),
                                r=[K("pT%d" % s2), K("wp")], w=[K("pe")])
                    kb.op("act", lambda e: e.activation(out=sgm[:], in_=gate[:].rearrange("p a b -> p (a b)"),
                                                        func=AF.Sigmoid), r=[K("gate")], w=[K("sgm")])
                    kb.op("dve", lambda e: e.tensor_tensor(out=vt[:], in0=pe_[:].rearrange("p a b -> p (a b)"),
                                                           in1=sgm[:], op=ALU.mult), r=[K("pe"), K("sgm")], w=[K("vt")])
                    self.epilogue(vt[:], K("vt"), xt[s][:, j, :], K("xt%d" % s), gpost, 1.0, tmp)
                kb.dma("sp", self.xd[b * TB:(b + 1) * TB, :].rearrange("(j p) d -> p j d", p=128), xt[s][:],
                       r=[K("xt%d" % s)], w=[("xd", b)], sem=K("xt%d" % s))

            load(0)
            if PIPE:
                A(0)
            for b in range(NB):
                if b + 1 < NB:
                    load(b + 1)
                if PIPE:
                    if b + 1 < NB:
                        A(b + 1)
                else:
                    A(b)
                C(b)
            kb.barrier()

    def setup_attn_consts(self):
        kb = self.kb
        nc = self.nc
        with ExitStack() as es:
            sb = lambda n, sh, dt: es.enter_context(nc.sbuf_tensor("ac%d" % self.kb.q["pe"].count + n, sh, dt))
            relb = sb("relb", [32, 16], F32)
            oh = sb("oh", [32, 128], F32)
            wt = sb("wt", [16, 384], F32)
            pT_ = es.enter_context(nc.psum_tensor("acps%d" % self.kb.q["pe"].count, [16, 128], F32))
            kb.dma("sp", relb[:], self.rel_bias[:, :], w=["relb"], sem="ac0")
            kb.dma("sp", oh[:], self.oh_in[:, :], w=["oh"], sem="ac0b")
            kb.op("pe", lambda e: e.matmul(pT_[:], lhsT=relb[:], rhs=oh[:], start=True, stop=True),
                  r=["relb", "oh"], w=["acps"])
            kb.op("dve", lambda e: e.memset(wt[:, 0:128], -30000.0), w=["wt"])
            kb.op("dve", lambda e: e.memset(wt[:, 256:384], 0.0), w=["wt"])
            kb.op("dve", lambda e: e.tensor_copy(out=wt[:, 128:256], in_=pT_[:]), r=["acps"], w=["wt"])
            kb.dma("sp", self.wscr[:, :], wt[:], r=["wt"], w=["wscr"], sem="ac1")
            for col in range(16):
                for which in range(2):
                    off = col * 384 + (1 if which == 0 else 129)
                    src = bass.AP(tensor=self.wscr_h, offset=off, ap=[[1, 128], [1, 128]])
                    kb.dma("pool", self.brev[:, col, which, :], src, r=["wscr"], w=["brev"], sem="ac2")
            kb.barrier()

    def phase_attn(self, layer, src):
        kb = self.kb
        nc = self.nc
        j_att = layer // 2
        lam_init = 0.8 - 0.6 * float(np.exp(-0.3 * layer))
        TB = 256
        NBt = S // TB
        NQB = DBG_NB or 8
        NH = int(os.environ.get("KDBG_NH", "8"))
        gi_pre = layer * 8 + 2
        tag = "a%d" % layer
        with ExitStack() as es:
            sb = lambda n, sh, dt: es.enter_context(nc.sbuf_tensor(tag + n, sh, dt))
            ps = lambda n, sh, dt=F32: es.enter_context(nc.psum_tensor(tag + n, sh, dt))
            K = lambda n: tag + n
            self.brev = sb("brev", [128, 16, 2, 128], BF16)
            self.setup_attn_consts()
            hT = sb("hT", [128, 8, S], BF16)
            xt = [sb("xt%d" % i, [128, 2, D], F32) for i in range(2)]
            hb = sb("hb", [128, 2, D], BF16)
            wq = [sb("wq%d" % i, [128, 8, 128], BF16) for i in range(2)]
            wk = [sb("wk%d" % i, [128, 8, 128], BF16) for i in range(2)]
            wv = [sb("wv%d" % i, [128, 8, 128], BF16) for i in range(2)]
            QT = [sb("QT%d" % m, [64, S], BF16) for m in range(2)]
            KT = [sb("KT%d" % m, [64, S], BF16) for m in range(2)]
            V1 = sb("V1", [128, 32, 130], BF16)
            ET = [sb("ET%d" % i, [128, 512], BF16) for i in range(4)]
            ot = [sb("ot%d" % i, [128, 4, 128], F32) for i in range(2)]
            Osb = [sb("Osb%d" % i, [128, 4, 129], F32) for i in range(2)]
            o1 = sb("o1", [128, 4, 128], F32)
            o2 = sb("o2", [128, 4, 128], F32)
            fst = sb("fst", [128, 16], F32)
            lamt = sb("lamt", [128, 256], F32)
            lamp = sb("lamp", [128, 128], F32)
            lams = sb("lams", [128, 8], F32)
            es_tp = ExitStack()
            tp = [es_tp.enter_context(nc.psum_tensor(tag + "tp%d" % i, [128, 4, TB], BF16)) for i in range(2)]

            kb.dma("sp", lamt[:], self.attn_lam[j_att].rearrange("a b -> (a b)").partition_broadcast(128),
                   w=[K("lamt")], sem=K("lam"))
            lv = lamt[:].rearrange("p (a b c) -> p a b c", a=2, b=2)
            kb.op("dve", lambda e: e.tensor_tensor(out=lamp[:].rearrange("p (a c) -> p a c", a=2), in0=lv[:, :, 0, :],
                                                   in1=lv[:, :, 1, :], op=ALU.mult), r=[K("lamt")], w=[K("lamp")])
            kb.op("dve", lambda e: e.tensor_reduce(out=lams[:, 0:2], in_=lamp[:].rearrange("p (a c) -> p a c", a=2),
                                                   axis=AX.X, op=ALU.add), r=[K("lamp")], w=[K("lams")])
            kb.op("act", lambda e: e.activation(out=lams[:, 2:4], in_=lams[:, 0:2], func=AF.Exp),
                  r=[K("lams")], w=[K("lams2")])
            kb.op("dve", lambda e: e.tensor_tensor(out=lams[:, 4:5], in0=lams[:, 3:4], in1=lams[:, 2:3],
                                                   op=ALU.subtract), r=[K("lams2")], w=[K("lams3")])
            kb.op("dve", lambda e: e.tensor_scalar(out=lams[:, 5:6], in0=lams[:, 4:5], scalar1=-lam_init, scalar2=None,
                                                   op0=ALU.add), r=[K("lams3")], w=[K("neglam")])
            kb.op("dve", lambda e: e.memset(V1[:, :, 128:130], 1.0), w=[K("V1")])

            def load(b):
                s = b % 2
                kb.dma("sp", xt[s][:], src[b * TB:(b + 1) * TB, :].rearrange("(j p) d -> p j d", p=128),
                       r=[("xd", b)], w=[K("xt%d" % s)], sem=K("xt%d" % s))
            load(0)
            for b in range(NBt):
                if b + 1 < NBt:
                    load(b + 1)
                s = b % 2
                for j in range(2):
                    self.prenorm_hb(xt[s][:, j, :], K("xt%d" % s), hb[:, j, :], K("hb"))
                self.transposes(hb, K("hb"), 2, 8, hT, K("hT"), b * TB, tp, [K("tp0"), K("tp1")],
                                lambda k0, n: self.gT[:, gi_pre, k0:k0 + n], "gT")

            kb.barrier()
            es_tp.close()
            NS = 4
            sT = [ps("sT%d" % i, [128, 512]) for i in range(NS)]
            Oa = [ps("O%d" % m, [128, 4, 256]) for m in range(2)]
            wsrc = self.attn_w_qkv[j_att].rearrange("(kc p) n -> p kc n", p=128)

            def loadw(h):
                s = h % 2
                kb.dma("pool", wq[s][:], wsrc[:, :, h * 128:(h + 1) * 128], w=[K("wq%d" % s)], sem=K("wq%d" % s))
                kb.dma("pool", wk[s][:], wsrc[:, :, D + h * 128:D + (h + 1) * 128], w=[K("wk%d" % s)], sem=K("wk%d" % s))
                kb.dma("pool", wv[s][:], wsrc[:, :, 2 * D + h * 128:2 * D + (h + 1) * 128], w=[K("wv%d" % s)], sem=K("wv%d" % s))

            loadw(0)
            cnt = 0
            for h in range(NH):
                if h + 1 < NH:
                    loadw(h + 1)
                s = h % 2
                for m in range(2):
                    for (wt_, wkey, dst, dkey, scale) in ((wq[s], K("wq%d" % s), QT[m], K("QT%d" % m), 0.125),
                                                        (wk[s], K("wk%d" % s), KT[m], K("KT%d" % m), 1.0)):
                        for tb in range(8):
                            bank = sT[cnt % NS]
                            bk = K("sT%d" % (cnt % NS))
                            cnt += 1
                            for kc in range(8):
                                kb.op("pe", lambda e, bank=bank, wt_=wt_, kc=kc, m=m, tb=tb: e.matmul(
                                    bank[0:64, :], lhsT=wt_[:, kc, m * 64:(m + 1) * 64], rhs=hT[:, kc, tb * 512:(tb + 1) * 512],
                                    start=(kc == 0), stop=(kc == 7)), r=[wkey, K("hT")], w=[bk])
                            kb.op("act", lambda e, bank=bank, dst=dst, tb=tb, scale=scale: e.activation(
                                out=dst[:, tb * 512:(tb + 1) * 512], in_=bank[0:64, :], func=AF.Copy, scale=scale),
                                r=[bk], w=[dkey])
                for t4 in range(8):
                    bank = sT[cnt % NS]
                    bk = K("sT%d" % (cnt % NS))
                    cnt += 1
                    for ti in range(4):
                        tt = t4 * 4 + ti
                        for kc in range(8):
                            kb.op("pe", lambda e, bank=bank, ti=ti, tt=tt, kc=kc, wvs=wv[s]: e.matmul(
                                bank[:, ti * 128:(ti + 1) * 128], lhsT=hT[:, kc, tt * 128:(tt + 1) * 128], rhs=wvs[:, kc, :],
                                start=(kc == 0), stop=(kc == 7)), r=[K("wv%d" % s), K("hT")], w=[bk])
                    kb.op("dve", lambda e, bank=bank, t4=t4: e.tensor_copy(
                        out=V1[:, t4 * 4:(t4 + 1) * 4, 0:128], in_=bank[:].rearrange("p (a b) -> p a b", a=4)),
                        r=[bk], w=[K("V1")])
                st = self.stt

                def emit_S(step):
                    nonlocal cnt
                    qb, m, kt = step["qb"], step["m"], step["kt"]
                    col = h * 2 + m
                    i = kt - 4 * qb
                    j0 = max(i, 0)
                    bank = sT[cnt % NS]
                    bk = K("sT%d" % (cnt % NS))
                    et = ET[cnt % 4]
                    ek = K("ET%d" % (cnt % 4))
                    cnt += 1
                    step["et"], step["ek"], step["j0"] = et, ek, j0
                    ksl = KT[m][:, kt * 128:(kt + 1) * 128]
                    j = j0
                    while j < 4:
                        delta = j - i
                        if delta in (0, 1):
                            c0, c1 = j * 128, (j + 1) * 128
                            kb.op("pe", lambda e, bank=bank, ksl=ksl, m=m, c0=c0, c1=c1, qb=qb: e.matmul(
                                bank[:, c0:c1], lhsT=ksl, rhs=QT[m][:, qb * 512 + c0:qb * 512 + c1],
                                start=True, stop=False), r=[K("KT%d" % m), K("QT%d" % m)], w=[bk])
                            kb.op("pe", lambda e, bank=bank, c0=c0, c1=c1, col=col, delta=delta: e.matmul(
                                bank[:, c0:c1], lhsT=self.jflip[:], rhs=self.brev[:, col, delta, :],
                                start=False, stop=True), r=["jflip", "brev"], w=[bk])
                            j += 1
                        else:
                            c0, c1 = j * 128, 512
                            kb.op("pe", lambda e, bank=bank, ksl=ksl, m=m, c0=c0, c1=c1, qb=qb: e.matmul(
                                bank[:, c0:c1], lhsT=ksl, rhs=QT[m][:, qb * 512 + c0:qb * 512 + c1],
                                start=True, stop=True), r=[K("KT%d" % m), K("QT%d" % m)], w=[bk])
                            j = 4
                    kb.op("act", lambda e, bank=bank, et=et, j0=j0: e.activation(
                        out=et[:, j0 * 128:512], in_=bank[:, j0 * 128:512], func=AF.Exp), r=[bk], w=[ek])

                def emit_PV(step):
                    qb, m, kt = step["qb"], step["m"], step["kt"]
                    et, ek, j0 = step["et"], step["ek"], step["j0"]
                    O = Oa[m]
                    Ok = K("O%d" % m)
                    for j in range(j0, 4):
                        kb.op("pe", lambda e, O=O, et=et, j=j, kt=kt: e.matmul(
                            O[:, j, 0:129], lhsT=et[:, j * 128:(j + 1) * 128], rhs=V1[:, kt, 0:129],
                            start=(kt == 0 and j in (0, 2)), stop=False, skip_group_check=True),
                            r=[ek, K("V1")], w=[Ok])
                    if kt == 4 * qb + 3:
                        kb.op("act", lambda e, O=O, m=m: e.copy(out=Osb[m][:], in_=O[:, :, 0:129]),
                              r=[Ok], w=[K("Osb%d" % m)])
                        if m == 1:
                            finalize(qb)

                def finalize(qb):
                    osl = ot[(h * 8 + qb) % 2]
                    okey = K("ot%d" % ((h * 8 + qb) % 2))
                    bc = lambda ap: ap.to_broadcast([128, 4, 128])
                    kb.op("dve", lambda e: e.reciprocal(out=fst[:, 0:4], in_=Osb[0][:, :, 128]), r=[K("Osb0")], w=[K("f0")])
                    kb.op("dve", lambda e: e.reciprocal(out=fst[:, 4:8], in_=Osb[1][:, :, 128]), r=[K("Osb1")], w=[K("f1")])
                    kb.op("dve", lambda e: e.tensor_scalar(out=fst[:, 4:8], in0=fst[:, 4:8], scalar1=lams[:, 5:6], scalar2=None,
                                                           op0=ALU.mult), r=[K("f1"), K("neglam")], w=[K("f1")])
                    kb.op("dve", lambda e: e.tensor_tensor(out=o1[:], in0=Osb[0][:, :, 0:128], in1=bc(fst[:, 0:4].unsqueeze(2)),
                                                           op=ALU.mult), r=[K("Osb0"), K("f0")], w=[K("o1")])
                    kb.op("dve", lambda e: e.tensor_tensor(out=o2[:], in0=Osb[1][:, :, 0:128], in1=bc(fst[:, 4:8].unsqueeze(2)),
                                                           op=ALU.mult), r=[K("Osb1"), K("f1")], w=[K("o2")])
                    kb.op("dve", lambda e: e.tensor_tensor(out=o1[:], in0=o1[:], in1=o2[:], op=ALU.add),
                          r=[K("o1"), K("o2")], w=[K("o1")])
                    kb.op("pool", lambda e: e.tensor_tensor(out=o2[:], in0=o1[:], in1=o1[:], op=ALU.mult),
                          r=[K("o1"), K("o2")], w=[K("o2")])
                    kb.op("dve", lambda e: e.tensor_reduce(out=fst[:, 8:12], in_=o2[:], axis=AX.X, op=ALU.add),
                          r=[K("o2")], w=[K("f2")])
                    kb.op("act", lambda e: e.activation(out=fst[:, 12:16], in_=fst[:, 8:12], func=AF.Sqrt,
                                                        scale=1.0 / 128, bias=self.epsb[:, 0:1]),
                          r=[K("f2"), "epsb"], w=[K("f3")])
                    kb.op("dve", lambda e: e.reciprocal(out=fst[:, 8:12], in_=fst[:, 12:16]), r=[K("f3"), K("f2")], w=[K("f2")])
                    kb.op("dve", lambda e, osl=osl: e.tensor_tensor(out=osl[:], in0=o1[:], in1=bc(fst[:, 8:12].unsqueeze(2)),
                                                                    op=ALU.mult), r=[K("o1"), K("f2")], w=[okey])
                    dst = self.od[qb * 512:(qb + 1) * 512, h * 128:(h + 1) * 128].rearrange("(j p) v -> p j v", p=128)
                    kb.dma("sp", dst, osl[:], r=[okey], w=[("od", qb)], sem=okey)

                steps = [dict(qb=qb, m=m, kt=kt) for qb in range(NQB) for m in range(2) for kt in range(4 * qb + 4)]
                LAG = int(os.environ.get("KLAG", "2"))
                pend = []
                for stp in steps:
                    emit_S(stp)
                    pend.append(stp)
                    if len(pend) > LAG:
                        emit_PV(pend.pop(0))
                while pend:
                    emit_PV(pend.pop(0))
            kb.barrier()

    def phase_attn_out(self, layer, start_w=None):
        kb = self.kb
        nc = self.nc
        j_att = layer // 2
        lam_init = 0.8 - 0.6 * float(np.exp(-0.3 * layer))
        TB = 256
        NB = DBG_NB * 2 if DBG_NB else (S // TB)
        tag = "ao%d" % layer
        with ExitStack() as es:
            sb = lambda n, sh, dt: es.enter_context(nc.sbuf_tensor(tag + n, sh, dt))
            ps = lambda n, sh, dt=F32: es.enter_context(nc.psum_tensor(tag + n, sh, dt))
            K = lambda n: tag + n
            wo = sb("wo", [128, 8, D], BF16)
            gpost = sb("gpost", [128, D], F32)
            sgn = sb("sgn", [128, 2], F32)
            xt = [sb("xt%d" % i, [128, 2, D], F32) for i in range(2)]
            ol = [sb("ol%d" % i, [128, 2, D], F32) for i in range(2)]
            hbs = [sb("hb%d" % i, [128, 2, D], BF16) for i in range(1)]
            hTs = [sb("hT%d" % i, [128, 8, TB], BF16) for i in range(2)]
            tmp = sb("tmp", [128, D], F32)
            tp = [ps("tp%d" % i, [128, 4, TB], BF16) for i in range(2)]
            ob = [ps("ob%d" % i, [128, 2, 512]) for i in range(2)]
            self.load_w(wo, K("wo"), self.attn_w_o[j_att].rearrange("(kc p) n -> p kc n", p=128), 2, D, K("wo"))
            kb.dma("sp", gpost[:], self.norm_g[layer, 3, :].partition_broadcast(128), w=["gpost"], sem=K("gp"))
            kb.dma("sp", sgn[:, 0:1], self.attn_subln_g[j_att].rearrange("(p o) -> p o", o=1), w=[K("sgn")], sem=K("sgn"))
            kb.op("dve", lambda e: e.tensor_scalar(out=sgn[:, 1:2], in0=sgn[:, 0:1], scalar1=1.0 - lam_init, scalar2=None,
                                                   op0=ALU.mult), r=[K("sgn")], w=[K("sgn2")])

            def load(b):
                s = b % 2
                kb.dma("sp", xt[s][:], self.xd[b * TB:(b + 1) * TB, :].rearrange("(j p) d -> p j d", p=128),
                       r=[("xd", b)], w=[K("xt%d" % s)], sem=K("xt%d" % s))
                kb.dma("sp", ol[s][:], self.od[b * TB:(b + 1) * TB, :].rearrange("(j p) d -> p j d", p=128),
                       r=[("od", b // 2)], w=[K("ol%d" % s)], sem=K("ol%d" % s))

            def A(b):
                s = b % 2
                hb, hT = hbs[0], hTs[s]
                kb.op("act", lambda e, s=s, hb=hb: e.copy(out=hb[:], in_=ol[s][:]), r=[K("ol%d" % s)], w=[K("hb0")])
                self.transposes(hb, K("hb0"), 2, 8, hT, K("hT%d" % s), 0, tp, [K("tp0"), K("tp1")],
                                lambda k0, n: sgn[:, 1:2].to_broadcast([128, n]), K("sgn2"))

            def Cmm(b):
                s = b % 2
                hT = hTs[s]
                for j in range(2):
                    o = ob[j]
                    ok = K("ob%d" % j)
                    for half in range(2):
                        for kc in range(8):
                            kb.op("pe", lambda e, o=o, half=half, kc=kc, j=j, hT=hT: e.matmul(
                                o[:, half, :], lhsT=hT[:, kc, j * 128:(j + 1) * 128],
                                rhs=wo[:, kc, half * 512:(half + 1) * 512], start=(kc == 0), stop=(kc == 7)),
                                r=[K("hT%d" % s), K("wo")], w=[ok])

            def Cepi(b):
                s = b % 2
                for j in range(2):
                    o = ob[j]
                    ok = K("ob%d" % j)
                    self.epilogue(o[:].rearrange("p a b -> p (a b)"), ok, xt[s][:, j, :], K("xt%d" % s), gpost, 1.0, tmp)
                kb.dma("sp", self.xd[b * TB:(b + 1) * TB, :].rearrange("(j p) d -> p j d", p=128), xt[s][:],
                       r=[K("xt%d" % s)], w=[("xd", b)], sem=K("xt%d" % s))

            if start_w is not None:
                start_w()
            load(0)
            A(0)
            for b in range(NB):
                if b + 1 < NB:
                    load(b + 1)
                Cmm(b)
                if b + 1 < NB:
                    A(b + 1)
                Cepi(b)
            kb.barrier()


    def cmul(self, eng, out_re, out_im, a_re, a_im, b_re, b_im, t1, t2, keys_r, key_w):
        kb = self.kb
        tk = "cmt"
        kb.op(eng, lambda e: e.tensor_tensor(out=t1, in0=a_re, in1=b_re, op=ALU.mult), r=keys_r, w=[tk + "1"])
        kb.op(eng, lambda e: e.tensor_tensor(out=t2, in0=a_im, in1=b_im, op=ALU.mult), r=keys_r, w=[tk + "2"])
        kb.op(eng, lambda e: e.tensor_tensor(out=out_re, in0=t1, in1=t2, op=ALU.subtract), r=[tk + "1", tk + "2"], w=[key_w + "r"])
        kb.op(eng, lambda e: e.tensor_tensor(out=t1, in0=a_re, in1=b_im, op=ALU.mult), r=keys_r + [key_w + "r"], w=[tk + "1"])
        kb.op(eng, lambda e: e.tensor_tensor(out=t2, in0=a_im, in1=b_re, op=ALU.mult), r=keys_r + [key_w + "r"], w=[tk + "2"])
        kb.op(eng, lambda e: e.tensor_tensor(out=out_im, in0=t1, in1=t2, op=ALU.add), r=[tk + "1", tk + "2"], w=[key_w + "i"])

    def phase_ssm(self, layer, src):
        kb = self.kb
        nc = self.nc
        js = layer // 2
        TB = 256
        NBt = S // TB
        NGP = DBG_NB or 32
        gi_pre = layer * 8 + 2
        tag = "s%d" % layer
        PI = float(np.pi)
        with ExitStack() as es:
            sb = lambda n, sh, dt: es.enter_context(nc.sbuf_tensor(tag + n, sh, dt))
            ps = lambda n, sh, dt=F32: es.enter_context(nc.psum_tensor(tag + n, sh, dt))
            K = lambda n: tag + n
            WinT = sb("WinT", [128, 64, 2, 64], BF16)
            Vre = sb("Vre", [128, 32, 128], BF16)
            Vim = sb("Vim", [128, 32, 128], BF16)
            T0 = sb("T0", [128, 64, 128], BF16)
            ALr = sb("ALr", [128, 9, 32], F32)
            ALi = sb("ALi", [128, 9, 32], F32)
            ALn = sb("ALn", [128, 9, 32], F32)
            es_tp = ExitStack()
            tp = [es_tp.enter_context(nc.psum_tensor(tag + "tp%d" % i, [128, 4, TB], BF16)) for i in range(2)]
            pT0 = [es_tp.enter_context(nc.psum_tensor(tag + "pt0%d" % i, [128, 128], F32)) for i in range(2)]
            with ExitStack() as es2:
                sb2 = lambda n, sh, dt: es2.enter_context(nc.sbuf_tensor(tag + n, sh, dt))
                lr = sb2("lr", [128, 32], F32)
                li = sb2("li", [128, 32], F32)
                ldt = sb2("ldt", [128, 32], F32)
                br = sb2("br", [128, 32, 16], F32)
                bi = sb2("bi", [128, 32, 16], F32)
                cr = sb2("cr", [128, 32, 16], F32)
                ci = sb2("ci", [128, 32, 16], F32)
                dcol = sb2("dcol", [128, 64], F32)
                maskt = sb2("maskt", [128, 128], F32)
                identf = sb2("identf", [128, 128], F32)
                sm = sb2("sm", [128, 24, 32], F32)
                P_re = sb2("Pre", [128, 9, 32], F32)
                P_im = sb2("Pim", [128, 9, 32], F32)
                bbr = sb2("bbr", [128, 32, 16], F32)
                bbi = sb2("bbi", [128, 32, 16], F32)
                w1 = sb2("w1", [128, 32, 16], F32)
                w2 = sb2("w2", [128, 32, 16], F32)
                E_re = sb2("Ere", [128, 32, 8, 16], F32)
                E_im = sb2("Eim", [128, 32, 8, 16], F32)
                F_re = sb2("Fre", [128, 32, 8, 16], F32)
                F_im = sb2("Fim", [128, 32, 8, 16], F32)
                Eb_re = sb2("Ebre", [128, 32, 128], BF16)
                Eb_im = sb2("Ebim", [128, 32, 128], BF16)
                Gb_re = sb2("Gbre", [128, 32, 128], BF16)
                Gb_im = sb2("Gbim", [128, 32, 128], BF16)
                t0s = sb2("t0s", [128, 128], F32)

                def dram(tensor_ap, off, ap):
                    return bass.AP(tensor=tensor_ap.tensor, offset=off, ap=ap)
                kb.dma("sp", lr[:], dram(self.ssm_lam_re, js * 4096, [[1, 128], [128, 32]]), w=[K("lr")], sem=K("lr"), slow=True)
                kb.dma("sp", li[:], dram(self.ssm_lam_im, js * 4096, [[1, 128], [128, 32]]), w=[K("li")], sem=K("li"), slow=True)
                for g2 in range(2):
                    kb.dma("sp", ldt[64 * g2:64 * g2 + 64, :], dram(self.ssm_log_dt, js * 64 + g2, [[0, 64], [2, 32]]),
                           w=[K("ldt")], sem=K("ldt"), slow=True)
                kb.dma("sp", br[:], dram(self.ssm_b_re, js * 65536, [[16, 128], [2048, 32], [1, 16]]), w=[K("br")], sem=K("br"))
                kb.dma("sp", bi[:], dram(self.ssm_b_im, js * 65536, [[16, 128], [2048, 32], [1, 16]]), w=[K("bi")], sem=K("bi"))
                for g2 in range(2):
                    for h_ in range(16):
                        off = js * 65536 + g2 * 1024 + h_ * 64
                        kb.dma("sp", cr[64 * g2:64 * g2 + 64, :, h_], dram(self.ssm_c_re, off, [[1, 64], [2048, 32]]),
                               w=[K("cr")], sem=K("cr"), slow=True)
                        kb.dma("sp", ci[64 * g2:64 * g2 + 64, :, h_], dram(self.ssm_c_im, off, [[1, 64], [2048, 32]]),
                               w=[K("ci")], sem=K("ci"), slow=True)
                for s_ in range(8):
                    kb.dma("sp", dcol[16 * s_:16 * s_ + 16, :], dram(self.ssm_d, js * 1024, [[1, 16], [16, 64]]),
                           w=[K("dcol")], sem=K("dcol"), slow=True)
                kb.dma("sp", maskt[:], self.mask_in[:, :], w=[K("maskt")], sem=K("maskt"))
                kb.dma("sp", identf[:], self.ident_in[:, :], w=[K("identf")], sem=K("identf"))

                V = lambda i: sm[:, i, :]
                T = lambda eng, fn, r, w: kb.op(eng, fn, r=r, w=w)
                tt = lambda out, a, b, op, r, w, eng="dve": kb.op(eng, lambda e: e.tensor_tensor(out=out, in0=a, in1=b, op=op), r=r, w=w)
                ts = lambda out, a, s1, s2, op0, op1, r, w: kb.op("dve", lambda e: e.tensor_scalar(
                    out=out, in0=a, scalar1=s1, scalar2=s2, op0=op0, op1=op1), r=r, w=w)
                kb.op("act", lambda e: e.activation(out=V(0), in_=ldt[:], func=AF.Exp), r=[K("ldt")], w=[K("dt")])
                tt(V(1), lr[:], V(0), ALU.mult, [K("lr"), K("dt")], [K("ar")])
                tt(V(2), li[:], V(0), ALU.mult, [K("li"), K("dt")], [K("ai")])
                kb.op("act", lambda e: e.activation(out=V(3), in_=V(1), func=AF.Exp), r=[K("ar")], w=[K("mag")])
                kb.op("dve", lambda e: e.tensor_copy(out=V(5), in_=V(2)), r=[K("ai")], w=[K("rr")])
                for m_ in range(1, 7):
                    thr = (2 * m_ - 1) * PI
                    kb.op("dve", lambda e, thr=thr: e.tensor_scalar(out=V(4), in0=V(2), scalar1=thr, scalar2=-2 * PI,
                                                                    op0=ALU.is_ge, op1=ALU.mult), r=[K("ai"), K("rr")], w=[K("rm")])
                    kb.op("dve", lambda e: e.tensor_tensor(out=V(5), in0=V(5), in1=V(4), op=ALU.add),
                          r=[K("rm"), K("rr")], w=[K("rr")])
                kb.op("act", lambda e: e.activation(out=V(6), in_=V(5), func=AF.Sin), r=[K("rr")], w=[K("sin")])
                kb.op("act", lambda e: e.activation(out=V(7), in_=V(5), func=AF.Sin, scale=0.5), r=[K("rr")], w=[K("sinh")])
                tt(V(8), V(7), V(7), ALU.mult, [K("sinh")], [K("sh2")])
                ts(V(9), V(8), -2.0, 1.0, ALU.mult, ALU.add, [K("sh2")], [K("cos")])
                kb.op("dve", lambda e: e.memset(P_re[:, 0, :], 1.0), w=[K("P0r")])
                kb.op("dve", lambda e: e.memset(P_im[:, 0, :], 0.0), w=[K("P0i")])
                tt(P_re[:, 1, :], V(3), V(9), ALU.mult, [K("mag"), K("cos")], [K("P1r")])
                tt(P_im[:, 1, :], V(3), V(6), ALU.mult, [K("mag"), K("sin")], [K("P1i")])
                for k in range(1, 8):
                    self.cmul("dve", P_re[:, k + 1, :], P_im[:, k + 1, :], P_re[:, k, :], P_im[:, k, :],
                              P_re[:, 1, :], P_im[:, 1, :], V(10), V(11),
                              [K("P%dr" % k), K("P%di" % k), K("P1r"), K("P1i")], K("P%d" % (k + 1)))
                kb.op("dve", lambda e: e.tensor_copy(out=ALr[:, 0, :], in_=P_re[:, 8, :]), r=[K("P8r")], w=[K("AL0r")])
                kb.op("dve", lambda e: e.tensor_copy(out=ALi[:, 0, :], in_=P_im[:, 8, :]), r=[K("P8i")], w=[K("AL0i")])
                for l in range(8):
                    self.cmul("dve", ALr[:, l + 1, :], ALi[:, l + 1, :], ALr[:, l, :], ALi[:, l, :], ALr[:, l, :], ALi[:, l, :],
                              V(10), V(11), [K("AL%dr" % l), K("AL%di" % l)], K("AL%d" % (l + 1)))
                kb.op("dve", lambda e: e.tensor_scalar(out=ALn[:], in0=ALi[:], scalar1=-1.0, scalar2=None, op0=ALU.mult),
                      r=[K("AL%di" % l) for l in range(9)], w=[K("ALn")])
                allAL = [K("AL%dr" % l) for l in range(9)] + [K("AL%di" % l) for l in range(9)] + [K("ALn")]
                ts(V(12), P_re[:, 1, :], -1.0, None, ALU.add, None, [K("P1r")], [K("am1")]) if False else kb.op(
                    "dve", lambda e: e.tensor_scalar(out=V(12), in0=P_re[:, 1, :], scalar1=-1.0, scalar2=None, op0=ALU.add),
                    r=[K("P1r")], w=[K("am1")])
                tt(V(13), lr[:], lr[:], ALU.mult, [K("lr")], [K("d2a")])
                tt(V(14), li[:], li[:], ALU.mult, [K("li")], [K("d2b")])
                tt(V(13), V(13), V(14), ALU.add, [K("d2a"), K("d2b")], [K("d2")])
                kb.op("dve", lambda e: e.reciprocal(out=V(15), in_=V(13)), r=[K("d2")], w=[K("inv")])
                tt(V(16), V(12), lr[:], ALU.mult, [K("am1"), K("lr")], [K("n1")])
                tt(V(17), P_im[:, 1, :], li[:], ALU.mult, [K("P1i"), K("li")], [K("n2")])
                tt(V(16), V(16), V(17), ALU.add, [K("n1"), K("n2")], [K("n3")])
                tt(V(18), V(16), V(15), ALU.mult, [K("n3"), K("inv")], [K("c1r")])
                tt(V(19), P_im[:, 1, :], lr[:], ALU.mult, [K("P1i"), K("lr")], [K("m1")])
                tt(V(20), V(12), li[:], ALU.mult, [K("am1"), K("li")], [K("m2")])
                tt(V(19), V(19), V(20), ALU.subtract, [K("m1"), K("m2")], [K("m3")])
                tt(V(21), V(19), V(15), ALU.mult, [K("m3"), K("inv")], [K("c1i")])
                bc16 = lambda ap: ap.unsqueeze(2).to_broadcast([128, 32, 16])
                w1v = w1[:]
                w2v = w2[:]
                self.cmul("dve", bbr[:], bbi[:], bc16(V(18)), bc16(V(21)), br[:], bi[:], w1v, w2v,
                          [K("c1r"), K("c1i"), K("br"), K("bi")], K("bb"))
                for s_ in range(8):
                    k = 7 - s_
                    self.cmul("dve", E_re[:, :, s_, :], E_im[:, :, s_, :], bc16(P_re[:, k, :]), bc16(P_im[:, k, :]),
                              bbr[:], bbi[:], w1v, w2v, [K("P%dr" % k), K("P%di" % k), K("bbr"), K("bbi")], K("E%d" % s_))
                for t_ in range(8):
                    k = t_ + 1
                    self.cmul("dve", F_re[:, :, t_, :], F_im[:, :, t_, :], bc16(P_re[:, k, :]), bc16(P_im[:, k, :]),
                              cr[:], ci[:], w1v, w2v, [K("P%dr" % k), K("P%di" % k), K("cr"), K("ci")], K("F%d" % t_))
                allE = [K("E%d%s" % (i, c)) for i in range(8) for c in "ri"]
                allF = [K("F%d%s" % (i, c)) for i in range(8) for c in "ri"]
                fl = lambda t4: t4[:].rearrange("p g s h -> p g (s h)")
                kb.op("act", lambda e: e.copy(out=Eb_re[:], in_=fl(E_re)), r=allE, w=[K("Ebre")])
                kb.op("act", lambda e: e.copy(out=Eb_im[:], in_=fl(E_im)), r=allE, w=[K("Ebim")])
                kb.op("act", lambda e: e.copy(out=Vre[:], in_=fl(F_re)), r=allF, w=[K("Vre")])
                kb.op("act", lambda e: e.activation(out=Vim[:], in_=fl(F_im), func=AF.Copy, scale=-1.0), r=allF, w=[K("Vim")])
                tt(V(13), P_re[:, 8, :], P_re[:, 8, :], ALU.mult, [K("P8r"), K("inv")], [K("q1")])
                tt(V(14), P_im[:, 8, :], P_im[:, 8, :], ALU.mult, [K("P8i"), K("inv")], [K("q2")])
                tt(V(13), V(13), V(14), ALU.add, [K("q1"), K("q2")], [K("q3")])
                kb.op("dve", lambda e: e.reciprocal(out=V(15), in_=V(13)), r=[K("q3")], w=[K("i8")])
                tt(V(22), P_re[:, 8, :], V(15), ALU.mult, [K("P8r"), K("i8")], [K("nr")])
                tt(V(23), ALn[:, 0, :], V(15), ALU.mult, [K("ALn"), K("i8")], [K("ni")])
                w1g = w1[:].rearrange("p g h -> p (g h)").rearrange("p (a b) -> p a b", a=4)
                w2g = w2[:].rearrange("p g h -> p (g h)").rearrange("p (a b) -> p a b", a=4)
                for q4 in range(8):
                    g0, g1 = q4 * 4, q4 * 4 + 4
                    bcg = lambda ap: ap[:, g0:g1].unsqueeze(2).to_broadcast([128, 4, 128])
                    self.cmul("dve", Gb_re[:, g0:g1, :], Gb_im[:, g0:g1, :], bcg(V(22)), bcg(V(23)),
                              fl(E_re)[:, g0:g1, :], fl(E_im)[:, g0:g1, :], w1g, w2g,
                              [K("nr"), K("ni")] + allE, K("Gb%d" % q4))
                allG = [K("Gb%d%s" % (q4, c)) for q4 in range(8) for c in "ri"]
                cntp = 0
                for g in range(64):
                    gp, g2 = g // 2, g % 2
                    t = tp[cntp % 2]
                    tk = K("tp%d" % (cntp % 2))
                    cntp += 1
                    for ri, Eb, ek in ((0, Eb_re, K("Ebre")), (1, Eb_im, K("Ebim"))):
                        kb.op("pe", lambda e, t=t, ri=ri, Eb=Eb, gp=gp, g2=g2: e.transpose(
                            out=t[:, ri, 0:64], in_=Eb[64 * g2:64 * g2 + 64, gp, :],
                            identity=self.ident[64 * g2:64 * g2 + 64, 64 * g2:64 * g2 + 64]), r=[ek, "ident"], w=[tk])
                    kb.op("act", lambda e, t=t, g=g: e.copy(out=WinT[:, g, :, :], in_=t[:, 0:2, 0:64]), r=[tk], w=[K("WinT")])
                    pt = pT0[g % 2]
                    pk = K("pt0%d" % (g % 2))
                    kb.op("pe", lambda e, pt=pt, gp=gp, g2=g2: e.matmul(
                        pt[:], lhsT=Gb_re[64 * g2:64 * g2 + 64, gp, :], rhs=Vre[64 * g2:64 * g2 + 64, gp, :],
                        start=True, stop=False), r=allG + [K("Vre")], w=[pk])
                    kb.op("pe", lambda e, pt=pt, gp=gp, g2=g2: e.matmul(
                        pt[:], lhsT=Gb_im[64 * g2:64 * g2 + 64, gp, :], rhs=Vim[64 * g2:64 * g2 + 64, gp, :],
                        start=False, stop=True), r=allG + [K("Vim")], w=[pk])
                    kb.op("dve", lambda e, pt=pt: e.tensor_tensor(out=t0s[:], in0=pt[:], in1=maskt[:], op=ALU.mult),
                          r=[pk, K("maskt")], w=[K("t0s")])
                    kb.op("dve", lambda e, g=g: e.scalar_tensor_tensor(
                        out=T0[:, g, :], in0=identf[:], scalar=dcol[:, g:g + 1], in1=t0s[:], op0=ALU.mult, op1=ALU.add),
                        r=[K("identf"), K("dcol"), K("t0s")], w=[K("T0")])
                kb.barrier()
            hT = sb("hT8", [128, 8, 8, 512], BF16)
            with ExitStack() as es3:
                xt = [es3.enter_context(nc.sbuf_tensor(tag + "xt%d" % i, [128, 2, D], F32)) for i in range(2)]
                hb = es3.enter_context(nc.sbuf_tensor(tag + "hb", [128, 2, D], BF16))

                def load(b):
                    s = b % 2
                    kb.dma("sp", xt[s][:], src[b * TB:(b + 1) * TB, :].rearrange("(j p) d -> p j d", p=128),
                           r=[("xd", b)], w=[K("xt%d" % s)], sem=K("xt%d" % s))
                load(0)
                for b in range(NBt):
                    if b + 1 < NBt:
                        load(b + 1)
                    s = b % 2
                    for j in range(2):
                        self.prenorm_hb(xt[s][:, j, :], K("xt%d" % s), hb[:, j, :], K("hb"))
                    self.transposes(hb, K("hb"), 2, 8, hT, K("hT"), b * TB, tp, [K("tp0"), K("tp1")],
                                    lambda k0, n: self.gT[:, gi_pre, k0:k0 + n], "gT", layout8=True)
                for c_ in range(8):
                    kb.dma("sp", self.hd[c_, :, :, :], hT[:, c_, :, :], r=[K("hT")], w=["hd"], sem=K("hd%d" % (c_ % 2)))
                kb.barrier()
            es_tp.close()
            NR = 16
            NSL = 4
            Rg = [sb("Rg%d" % i, [128, 512], BF16) for i in range(NR)]
            X_ = [[[sb("X%d%d%d" % (sl, ab, ri), [128, 768], F32) for ri in range(2)] for ab in range(2)] for sl in range(NSL)]
            for sl_ in range(NSL):
                for ab_ in range(2):
                    for ri_ in range(2):
                        kb.op("dve", lambda e, t_=X_[sl_][ab_][ri_]: e.memset(t_[:, 0:256], 0.0), w=[K("Xpad")])
            Xh = [[sb("Xh%d%d" % (sl, ri), [128, 512], BF16) for ri in range(2)] for sl in range(NSL)]
            yg = [[sb("yg%d%d" % (sl, i), [128, 512], BF16) for i in range(2)] for sl in range(NSL)]
            xin = [[ps("xin%d%d" % (sl, i), [128, 512]) for i in range(2)] for sl in range(2)]
            yps = [[ps("yps%d%d" % (sl, i), [128, 512]) for i in range(2)] for sl in range(2)]

            def loadR(g):
                slot = g % NR
                c, g8 = g // 8, g % 8
                kb.dma("sp", Rg[slot][:, :], self.hd[c, 16 * g8:16 * g8 + 16, :, :].rearrange("h s b -> s h b"),
                       r=["hd"], w=[K("Rg%d" % slot)], sem=K("Rg%d" % slot))

            def pair_gen(gp, sl):
                psl = sl % 2
                Xk = lambda ab, ri: K("X%d%d%d" % (sl, ab, ri))
                for g2 in range(2):
                    g = 2 * gp + g2
                    R = Rg[g % NR]
                    rk = K("Rg%d" % (g % NR))
                    for ri in range(2):
                        kb.op("pe", lambda e, ri=ri, g=g, g2=g2, R=R: e.matmul(
                            xin[psl][ri][64 * g2:64 * g2 + 64, :], lhsT=WinT[:, g, ri, :], rhs=R[:, :],
                            start=True, stop=True), r=[K("WinT"), rk], w=[K("xin%d%d" % (psl, ri))])
                for ri in range(2):
                    kb.op("act", lambda e, ri=ri: e.copy(out=X_[sl][0][ri][:, 256:768], in_=xin[psl][ri][:]),
                          r=[K("xin%d%d" % (psl, ri)), K("Xpad")], w=[Xk(0, ri)])
                yield
                ab = 0
                for l in range(9):
                    d = 1 << l
                    cur, nxt = X_[sl][ab], X_[sl][1 - ab]
                    ar = ALr[:, l, gp:gp + 1]
                    ai = ALi[:, l, gp:gp + 1]
                    an = ALn[:, l, gp:gp + 1]
                    rk_ = [Xk(ab, 0), Xk(ab, 1)] + allAL
                    kb.op("dve", lambda e, cur=cur, nxt=nxt, d=d, ar=ar: e.scalar_tensor_tensor(
                        out=nxt[0][:, 256:768], in0=cur[0][:, 256 - d:768 - d], scalar=ar, in1=cur[0][:, 256:768],
                        op0=ALU.mult, op1=ALU.add), r=rk_, w=[Xk(1 - ab, 0)])
                    kb.op("dve", lambda e, cur=cur, nxt=nxt, d=d, ar=ar: e.scalar_tensor_tensor(
                        out=nxt[1][:, 256:768], in0=cur[1][:, 256 - d:768 - d], scalar=ar, in1=cur[1][:, 256:768],
                        op0=ALU.mult, op1=ALU.add), r=rk_, w=[Xk(1 - ab, 1)])
                    yield
                    kb.op("dve", lambda e, cur=cur, nxt=nxt, d=d, an=an: e.scalar_tensor_tensor(
                        out=nxt[0][:, 256:768], in0=cur[1][:, 256 - d:768 - d], scalar=an, in1=nxt[0][:, 256:768],
                        op0=ALU.mult, op1=ALU.add), r=rk_ + [Xk(1 - ab, 0)], w=[Xk(1 - ab, 0)])
                    kb.op("dve", lambda e, cur=cur, nxt=nxt, d=d, ai=ai: e.scalar_tensor_tensor(
                        out=nxt[1][:, 256:768], in0=cur[0][:, 256 - d:768 - d], scalar=ai, in1=nxt[1][:, 256:768],
                        op0=ALU.mult, op1=ALU.add), r=rk_ + [Xk(1 - ab, 1)], w=[Xk(1 - ab, 1)])
                    yield
                    ab = 1 - ab
                fin = X_[sl][ab]
                for ri in range(2):
                    kb.op("act", lambda e, ri=ri, fin=fin: e.copy(out=Xh[sl][ri][:], in_=fin[ri][:, 256:768]),
                          r=[Xk(ab, ri)], w=[K("Xh%d%d" % (sl, ri))])
                yield
                for g2 in range(2):
                    g = 2 * gp + g2
                    R = Rg[g % NR]
                    rk = K("Rg%d" % (g % NR))
                    yp = yps[psl][g2]
                    yk = K("yps%d%d" % (psl, g2))
                    kb.op("pe", lambda e, yp=yp, g=g, R=R: e.matmul(
                        yp[:, :], lhsT=T0[:, g, :], rhs=R[:, :], start=True, stop=False), r=[K("T0"), rk], w=[yk])
                    kb.op("pe", lambda e, yp=yp, g2=g2: e.matmul(
                        yp[:, 1:512], lhsT=Vre[64 * g2:64 * g2 + 64, gp, :], rhs=Xh[sl][0][64 * g2:64 * g2 + 64, 0:511],
                        start=False, stop=False), r=[K("Vre"), K("Xh%d0" % sl)], w=[yk])
                    kb.op("pe", lambda e, yp=yp, g2=g2: e.matmul(
                        yp[:, 1:512], lhsT=Vim[64 * g2:64 * g2 + 64, gp, :], rhs=Xh[sl][1][64 * g2:64 * g2 + 64, 0:511],
                        start=False, stop=True), r=[K("Vim"), K("Xh%d1" % sl)], w=[yk])
                    ygt = yg[sl][g2]
                    ygk = K("yg%d%d" % (sl, g2))
                    kb.op("act", lambda e, yp=yp, ygt=ygt: e.activation(out=ygt[:], in_=yp[:], func=AF.Gelu),
                          r=[yk], w=[ygk])
                    c, g8 = g // 8, g % 8
                    kb.dma("sp", self.yd[c, 16 * g8:16 * g8 + 16, :, :].rearrange("h t b -> t h b"), ygt[:, :],
                           r=[ygk], w=["yd"], sem=ygk)
                    yield

            for g in range(min(2 * NSL, 2 * NGP)):
                loadR(g)
            for gp0 in range(0, NGP, NSL):
                for g in range(2 * gp0 + 2 * NSL, min(2 * gp0 + 4 * NSL, 2 * NGP)):
                    loadR(g)
                gens = [pair_gen(gp0 + i, i) for i in range(NSL) if gp0 + i < NGP]
                while gens:
                    for gen in list(gens):
                        try:
                            next(gen)
                        except StopIteration:
                            gens.remove(gen)
            kb.barrier()

    def phase_glu(self, layer):
        kb = self.kb
        nc = self.nc
        js = layer // 2
        TB = 256
        NB = S // TB
        tag = "gl%d" % layer
        with ExitStack() as es:
            sb = lambda n, sh, dt: es.enter_context(nc.sbuf_tensor(tag + n, sh, dt))
            ps = lambda n, sh, dt=F32: es.enter_context(nc.psum_tensor(tag + n, sh, dt))
            K = lambda n: tag + n
            wgl = sb("wgl", [128, 8, 2 * D], BF16)
            bgl = sb("bgl", [1, 2 * D], BF16)
            ones = sb("ones", [1, 128], BF16)
            gpost = sb("gpost", [128, D], F32)
            xt = [sb("xt%d" % i, [128, 2, D], F32) for i in range(2)]
            sgm = sb("sgm", [128, D], F32)
            vt = sb("vt", [128, D], F32)
            tmp = sb("tmp", [128, D], F32)
            zp = [ps("zp%d" % i, [128, 4, 512]) for i in range(2)]
            ysb = [sb("ysb%d" % i, [128, 8, 8, 128], BF16) for i in range(2)]
            lc = [sb("lc%d" % i, [128, 8, 128], BF16) for i in range(2)]

            def loady(sbk):
                sl = sbk % 2
                for c_ in range(8):
                    kb.dma("sp", ysb[sl][:, c_, :, :], self.yd[c_, :, :, sbk * 128:(sbk + 1) * 128],
                           r=["yd"], w=[K("ysb%d" % sl)], sem=K("ysb%d" % sl))
            loady(0)
            self.load_w(wgl, K("wgl"), self.ssm_w_glu[js].rearrange("(kc p) n -> p kc n", p=128), 4, 2 * D, K("wgl"))
            kb.dma("pool", bgl[:], self.ssm_b_glu[js:js + 1, :], w=[K("bgl")], sem=K("bgl"))
            kb.op("dve", lambda e: e.memset(ones[:], 1.0), w=[K("ones")])
            kb.dma("sp", gpost[:], self.norm_g[layer, 3, :].partition_broadcast(128), w=["gpost"], sem=K("gp"))

            def load(b):
                s = b % 2
                kb.dma("sp", xt[s][:], self.xd[b * TB:(b + 1) * TB, :].rearrange("(j p) d -> p j d", p=128),
                       r=[("xd", b)], w=[K("xt%d" % s)], sem=K("xt%d" % s))
            def ctx(k):
                b, j = k // 2, k % 2
                return b, j, b % 2, ysb[(b // 4) % 2], K("ysb%d" % ((b // 4) % 2)), k % 8

            def copies(k):
                b, j, s, ys, ysk, tl = ctx(k)
                lcj, lck = lc[j], K("lc%d" % j)
                for kc in range(8):
                    if kc % 2 == 0:
                        kb.op("dve", lambda e, lcj=lcj, kc=kc, ys=ys, tl=tl: e.tensor_copy(
                            out=lcj[:, kc, :].rearrange("p (b t) -> p b t", b=16),
                            in_=ys[:, kc, :, 16 * tl:16 * tl + 16].rearrange("p t b -> p b t")), r=[ysk], w=[lck])
                    else:
                        kb.op("act", lambda e, lcj=lcj, kc=kc, ys=ys, tl=tl: e.copy(
                            out=lcj[:, kc, :].rearrange("p (b t) -> p b t", b=16),
                            in_=ys[:, kc, :, 16 * tl:16 * tl + 16].rearrange("p t b -> p b t")), r=[ysk], w=[lck])

            def mm(k):
                b, j, s, ys, ysk, tl = ctx(k)
                z, zk, lcj, lck = zp[j], K("zp%d" % j), lc[j], K("lc%d" % j)
                for ch in range(4):
                    for kc in range(8):
                        kb.op("pe", lambda e, z=z, ch=ch, kc=kc, lcj=lcj: e.matmul(
                            z[:, ch, :], lhsT=lcj[:, kc, :], rhs=wgl[:, kc, ch * 512:(ch + 1) * 512],
                            start=(kc == 0), stop=False), r=[lck, K("wgl")], w=[zk])
                    kb.op("pe", lambda e, z=z, ch=ch: e.matmul(
                        z[:, ch, :], lhsT=ones[:, :], rhs=bgl[:, ch * 512:(ch + 1) * 512], start=False, stop=True),
                        r=[K("ones"), K("bgl")], w=[zk])

            def post(k):
                b, j, s, ys, ysk, tl = ctx(k)
                z, zk = zp[j], K("zp%d" % j)
                kb.op("act", lambda e, z=z: e.activation(out=sgm[:], in_=z[:, 2:4, :].rearrange("p a b -> p (a b)"),
                                                         func=AF.Sigmoid), r=[zk], w=[K("sgm")])
                kb.op("dve", lambda e, z=z: e.tensor_tensor(out=vt[:], in0=z[:, 0:2, :].rearrange("p a b -> p (a b)"),
                                                            in1=sgm[:], op=ALU.mult), r=[zk, K("sgm")], w=[K("vt")])
                self.epilogue(vt[:], K("vt"), xt[s][:, j, :], K("xt%d" % s), gpost, 1.0, tmp)
                if j == 1:
                    kb.dma("sp", self.xd[b * TB:(b + 1) * TB, :].rearrange("(j p) d -> p j d", p=128), xt[s][:],
                           r=[K("xt%d" % s)], w=[("xd", b)], sem=K("xt%d" % s))

            NK = NB * 2
            load(0)
            copies(0)
            for k in range(NK):
                b, j = k // 2, k % 2
                if j == 0:
                    if b + 1 < NB:
                        load(b + 1)
                    if b % 4 == 0 and b // 4 + 1 < 4:
                        loady(b // 4 + 1)
                mm(k)
                if k + 1 < NK:
                    copies(k + 1)
                post(k)
            kb.barrier()


DEFAULT_PHASES = []
for _i in range(DEPTH):
    DEFAULT_PHASES.append(("ffn", _i, 0))
    DEFAULT_PHASES.append(("attn", _i) if _i % 2 == 0 else ("ssm", _i))
    DEFAULT_PHASES.append(("ffn", _i, 1))
    DEFAULT_PHASES.append(("ple", _i))
_cache = {}


def get_prog(phases):
    key = tuple(phases)
    if key not in _cache:
        pr = Prog(list(phases))
        pr.build()
        _cache[key] = pr
    return _cache[key]


def t5_onehot():
    n = np.arange(128)
    nf = np.maximum(n, 1).astype(np.float32)
    large = 16 + (np.log(nf / np.float32(16)) / np.float32(np.log(128 / 16)) * np.float32(16)).astype(np.int32)
    large = np.minimum(large, 31)
    bucket = np.where(n < 16, n, large)
    oh = np.zeros((32, 128), np.float32)
    oh[bucket, n] = 1.0
    oh[31, :] -= 1.0
    return oh


def host_consts():
    return {"ident_in": np.eye(128, dtype=np.float32),
            "jflip_in": np.ascontiguousarray(np.eye(128, dtype=np.float32)[::-1]),
            "oh_in": t5_onehot(),
            "mask_in": np.kron((np.arange(8)[None, :] >= np.arange(8)[:, None]).astype(np.float32),
                               np.ones((16, 16), np.float32))}


def kernel(phases=None, **inputs):
    phases = DEFAULT_PHASES if phases is None else phases
    pr = get_prog(phases)
    f = lambda a: np.ascontiguousarray(np.asarray(a, dtype=np.float32))
    x = f(inputs["x"])
    p = f(inputs["p"])
    shared = {
        "norm_g": f(inputs["norm_g"]),
        "ffn_w_in": f(inputs["ffn_w_in"]),
        "ffn_w_out": f(inputs["ffn_w_out"]),
        "ple_w_proj": f(inputs["ple_w_proj"]),
        "ple_w_gate": f(inputs["ple_w_gate"]),
    }
    for k in ("attn_w_qkv", "attn_w_o", "attn_lam", "attn_subln_g", "rel_bias", "ssm_lam_re", "ssm_lam_im",
              "ssm_log_dt", "ssm_b_re", "ssm_b_im", "ssm_c_re", "ssm_c_im", "ssm_d", "ssm_w_glu", "ssm_b_glu"):
        shared[k] = f(inputs[k])
    shared.update(host_consts())
    in_maps = []
    for c in range(NCORES):
        m = dict(shared)
        m["x"] = x[c]
        m["p"] = np.ascontiguousarray(p[:, c])
        in_maps.append(m)
    res = run_bass_kernel_spmd(pr.nc, in_maps, core_ids=list(range(NCORES)))
    return np.stack([res.results[c]["out"] for c in range(NCORES)], axis=0)
```

```python
from contextlib import ExitStack
import os
import numpy as np
import concourse.bass as bass
import concourse.mybir as mybir
from concourse.bass_utils import run_bass_kernel_spmd

F32 = mybir.dt.float32
BF16 = mybir.dt.bfloat16
AF = mybir.ActivationFunctionType
ALU = mybir.AluOpType
AX = mybir.AxisListType

D = 1024
S = 4096
DFF = 2816
NFC = DFF // 128
DEPTH = 4
PLE = 256
EPS = 1e-6
NCORES = 8
DBG_NB = int(os.environ.get('KDBG_NB', '0'))
ATTACH_WAIT = int(os.environ.get('KATTACH', '1'))


class Res:
    __slots__ = ("lastw", "readers")

    def __init__(self):
        self.lastw = None
        self.readers = {}


class EngQ:
    def __init__(self, name, sem):
        self.name = name
        self.sem = sem
        self.count = 0
        self.waited = {}
        self.ops = []


class KB:
    def __init__(self):
        self.nc = bass.Bass("TRN2", target_bir_lowering=False)
        nc = self.nc
        self.q = {}
        for n in ("pe", "act", "dve", "pool", "sp"):
            self.q[n] = EngQ(n, nc.alloc_semaphore("prog_" + n))
        self.res = {}
        self.dsem = {}
        self.dcount = {}
        self.free_sems = {}
        self.sem_eng = {}
        self.all_sems = []
        self.stc = 0

    def _res(self, key):
        r = self.res.get(key)
        if r is None:
            r = self.res[key] = Res()
        return r

    def _collect(self, q, r, w):
        need = {}

        def add(tok, same_ok):
            if tok is None:
                return
            sem, val = tok
            if sem is q.sem and not same_ok:
                return
            if need.get(sem, 0) < val:
                need[sem] = val

        raw_same = q.name in ("act", "dve", "pool")
        for key in r:
            add(self._res(key).lastw, raw_same)
        for key in w:
            rs = self._res(key)
            add(rs.lastw, False)
            for sem, val in rs.readers.items():
                add((sem, val), raw_same)
        waits = []
        for sem, val in need.items():
            if q.waited.get(sem, 0) >= val:
                continue
            q.waited[sem] = val
            waits.append((sem, val))
        return waits

    def _update(self, tok, r, w):
        sem, val = tok
        for key in r:
            rs = self._res(key)
            if rs.readers.get(sem, 0) < val:
                rs.readers[sem] = val
        for key in w:
            rs = self._res(key)
            rs.lastw = tok
            rs.readers = {}

    def op(self, eng, fn, r=(), w=()):
        q = self.q[eng]
        waits = self._collect(q, r, w)
        q.count += 1
        tok = (q.sem, q.count)
        q.ops.append((waits, fn, (q.sem, 1)))
        self._update(tok, r, w)
        return tok

    def dma(self, eng, out, in_, r=(), w=(), sem=None, slow=False):
        q = self.q[eng]
        if sem not in self.dsem:
            fl = self.free_sems.setdefault(eng, [])
            if fl:
                self.dsem[sem] = fl.pop()
            else:
                sh = self.nc.alloc_semaphore("dma_%d" % len(self.all_sems))
                self.all_sems.append(sh)
                self.sem_eng[id(sh)] = eng
                self.dcount[id(sh)] = 0
                self.dsem[sem] = sh
        s = self.dsem[sem]
        waits = self._collect(q, r, w)
        self.dcount[id(s)] += 16
        tok = (s, self.dcount[id(s)])
        if slow:
            fn = lambda e, out=out, in_=in_: e.dma_start(out=out, in_=in_, allow_slow_non_contiguous=True)
        else:
            fn = lambda e, out=out, in_=in_: e.dma_start(out=out, in_=in_)
        q.ops.append((waits, fn, (s, 16)))
        self._update(tok, r, w)
        return tok

    def wait_all(self, eng, keys):
        q = self.q[eng]
        waits = self._collect(q, (), keys)
        q.ops.append((waits, None, None))

    def st(self):
        self.stc = (self.stc + 1) % 96
        return self.stc

    def barrier(self):
        toks = [(q.sem, q.count) for q in self.q.values() if q.count > 0]
        toks += [(sh, self.dcount[id(sh)]) for sh in self.all_sems if self.dcount[id(sh)] > 0]
        for q in self.q.values():
            waits = []
            for sem, val in toks:
                if sem is q.sem or q.waited.get(sem, 0) >= val:
                    continue
                q.waited[sem] = val
                waits.append((sem, val))
            q.ops.append((waits, None, None))
        self.free_sems = {}
        for sh in self.all_sems:
            self.free_sems.setdefault(self.sem_eng[id(sh)], []).append(sh)
        self.dsem = {}

    def emit(self):
        nc = self.nc
        with nc.Block() as block:
            def mk(qn):
                q = self.q[qn]

                def body(e):
                    for waits, fn, inc in q.ops:
                        attach = None
                        if ATTACH_WAIT and fn is not None and waits:
                            attach = waits[-1]
                            waits = waits[:-1]
                        for sem, val in waits:
                            e.wait_ge(sem, val)
                        if fn is not None:
                            ins = fn(e)
                            if attach is not None:
                                ins._wait_ge(attach[0], attach[1])
                            ins.then_inc(inc[0], inc[1])
                return body
            block.tensor(mk("pe"))
            block.scalar(mk("act"))
            block.vector(mk("dve"))
            block.gpsimd(mk("pool"))
            block.sync(mk("sp"))


class Prog:
    def __init__(self, phases):
        self.kb = KB()
        self.nc = self.kb.nc
        self.phases = phases
        self.es = ExitStack()
        self._held = None

    def sb(self, name, shape, dt):
        return self.es.enter_context(self.nc.sbuf_tensor(name, shape, dt))

    def ps(self, name, shape, dt=F32):
        return self.es.enter_context(self.nc.psum_tensor(name, shape, dt))

    def build(self):
        nc = self.nc
        kb = self.kb
        di = lambda n, sh: nc.dram_tensor(n, sh, F32, kind="ExternalInput").ap()
        self.x_in = di("x", [S, D])
        self.p_in = di("p", [DEPTH, S, PLE])
        self.norm_g = di("norm_g", [DEPTH, 8, D])
        self.ffn_w_in = di("ffn_w_in", [DEPTH, 2, D, 2 * DFF])
        self.ffn_w_out = di("ffn_w_out", [DEPTH, 2, DFF, D])
        self.ple_w_proj = di("ple_w_proj", [DEPTH, PLE, D])
        self.ple_w_gate = di("ple_w_gate", [DEPTH, D, D])
        self.ident_in = di("ident_in", [128, 128])
        self.jflip_in = di("jflip_in", [128, 128])
        self.oh_in = di("oh_in", [32, 128])
        self.attn_w_qkv = di("attn_w_qkv", [2, D, 3 * D])
        self.attn_w_o = di("attn_w_o", [2, D, D])
        self.attn_lam = di("attn_lam", [2, 4, 64])
        self.attn_subln_g = di("attn_subln_g", [2, 128])
        self.rel_bias = di("rel_bias", [32, 16])
        self.mask_in = di("mask_in", [128, 128])
        for nm, sh in (("ssm_lam_re", [2, 64, 64]), ("ssm_lam_im", [2, 64, 64]), ("ssm_log_dt", [2, 64]),
                       ("ssm_b_re", [2, 64, 64, 16]), ("ssm_b_im", [2, 64, 64, 16]), ("ssm_c_re", [2, 64, 16, 64]),
                       ("ssm_c_im", [2, 64, 16, 64]), ("ssm_d", [2, 1024]), ("ssm_w_glu", [2, D, 2 * D]),
                       ("ssm_b_glu", [2, 2 * D])):
            setattr(self, nm, di(nm, sh))
        self.yd = nc.dram_tensor("yd_scr", [8, 128, 8, 512], BF16).ap()
        self.hd = nc.dram_tensor("hd_scr", [8, 128, 8, 512], BF16).ap()
        self.od_h = nc.dram_tensor("od_scr", [S, D], F32)
        self.od = self.od_h.ap()
        self.wscr_h = nc.dram_tensor("w_scr", [16, 384], F32)
        self.wscr = self.wscr_h.ap()
        self.xd = nc.dram_tensor("out", [S, D], F32, kind="ExternalOutput").ap()

        with self.es:
            self.ident = self.sb("ident", [128, 128], BF16)
            kb.dma("pool", self.ident[:], self.ident_in[:, :], w=["ident"], sem="const")
            self.gT = self.sb("gT", [128, DEPTH * 8, 8], F32)
            nc_g = self.norm_g.rearrange("l n (kc p) -> p (l n) kc", p=128)
            kb.q["sp"].ops.append(([], None, None))
            kb.dma("sp", self.gT[:], nc_g, w=["gT"], sem="const2", slow=True)
            self.stt = self.sb("stats", [128, 96], F32)
            self.epsb = self.sb("epsb", [128, 1], F32)
            kb.op("dve", lambda e: e.memset(self.epsb[:], EPS), w=["epsb"])
            self.junk = self.sb("junk", [128, 1024], BF16)
            self.jflip = self.sb("jflip", [128, 128], BF16)
            kb.dma("pool", self.jflip[:], self.jflip_in[:, :], w=["jflip"], sem="ac3")

            first = True
            PREF = int(os.environ.get("KPREF", "1"))
            phs = self.phases
            pending = None
            for pi, ph in enumerate(phs):
                src = self.x_in if first else self.xd
                kind = ph[0]
                nxt = phs[pi + 1] if pi + 1 < len(phs) else None
                with ExitStack() as wes:
                    def prefetch_next():
                        if PREF and nxt is not None and nxt[0] == "ffn":
                            win, wout, start_w = self.ffn_weights(wes, nxt[1], nxt[2])
                            return (win, wout), start_w
                        return None, None
                    if kind == "ffn":
                        self.phase_ffn(ph[1], ph[2], src, pre=pending)
                        pending = None
                        if self._held is not None:
                            self._held.close()
                            self._held = None
                    elif kind == "ple":
                        pw, start_w = prefetch_next()
                        self.phase_ple(ph[1], src, start_w)
                        if pw is not None:
                            pending = pw
                            self._held = wes.pop_all()
                    elif kind == "attn":
                        assert not first
                        self.phase_attn(ph[1], src)
                        pw, start_w = prefetch_next()
                        self.phase_attn_out(ph[1], start_w)
                        if pw is not None:
                            pending = pw
                            self._held = wes.pop_all()
                    elif kind == "ssm":
                        assert not first
                        self.phase_ssm(ph[1], src)
                        self.phase_glu(ph[1])
                    elif kind == "copy":
                        for b in range(16):
                            kb.dma("sp", self.xd[b * 256:(b + 1) * 256, :], self.x_in[b * 256:(b + 1) * 256, :],
                                   w=[("xd", b)], sem="cp%d" % (b % 4))
                        kb.barrier()
                    else:
                        raise ValueError(kind)
                first = False
            keys = [("xd", b) for b in range(64)]
            kb.wait_all("sp", keys)
            kb.emit()
        return nc

    def rstd(self, src_ap, src_key, n_elem, extra_r=()):
        kb = self.kb
        st = self.stt
        c1, c2, c3 = kb.st(), kb.st(), kb.st()
        kb.op("act", lambda e: e.activation(out=self.junk[:, 0:n_elem], in_=src_ap, func=AF.Square,
                                            accum_out=st[:, c1:c1 + 1]),
              r=[src_key, *extra_r], w=["junk", ("st", c1)])
        kb.op("act", lambda e: e.activation(out=st[:, c2:c2 + 1], in_=st[:, c1:c1 + 1], func=AF.Sqrt,
                                            scale=1.0 / n_elem, bias=self.epsb[:, 0:1]),
              r=[("st", c1), "epsb"], w=[("st", c2)])
        kb.op("dve", lambda e: e.reciprocal(out=st[:, c3:c3 + 1], in_=st[:, c2:c2 + 1]),
              r=[("st", c2)], w=[("st", c3)])
        return c3

    def load_w(self, dst, dst_key, src_ap, nsplit, axis_len, sem):
        kb = self.kb
        step = axis_len // nsplit
        for i in range(nsplit):
            c0, c1 = i * step, (i + 1) * step
            kb.dma("pool", dst[:, :, c0:c1], src_ap[:, :, c0:c1], w=[dst_key], sem=sem)

    def epilogue(self, m_ap, m_key, xt_ap, xt_key, gpost, coef, tmp):
        kb = self.kb
        c = self.rstd(m_ap, m_key, D)
        st = self.stt
        kb.op("dve", lambda e: e.scalar_tensor_tensor(out=tmp[:], in0=m_ap, scalar=st[:, c:c + 1], in1=gpost[:],
                                                      op0=ALU.mult, op1=ALU.mult),
              r=[m_key, ("st", c), "gpost"], w=["tmp"])
        kb.op("dve", lambda e: e.scalar_tensor_tensor(out=xt_ap, in0=tmp[:], scalar=float(coef), in1=xt_ap,
                                                      op0=ALU.mult, op1=ALU.add),
              r=["tmp", xt_key], w=[xt_key])

    def ffn_weights(self, es, layer, which):
        nc = self.nc
        tag = "f%d%d" % (layer, which)
        win = es.enter_context(nc.sbuf_tensor(tag + "win", [128, 8, 2 * DFF], BF16))
        wout = es.enter_context(nc.sbuf_tensor(tag + "wout", [128, NFC, D], BF16))

        def start():
            self.load_w(win, tag + "win", self.ffn_w_in[layer, which].rearrange("(kc p) n -> p kc n", p=128),
                        8, 2 * DFF, tag + "win")
            self.load_w(wout, tag + "wout", self.ffn_w_out[layer, which].rearrange("(fc p) n -> p fc n", p=128),
                        4, D, tag + "wout")
        return (win, wout, start)

    def phase_ffn(self, layer, which, src, pre=None):
        kb = self.kb
        nc = self.nc
        TB = 256
        NB = DBG_NB or (S // TB)
        gi_pre = layer * 8 + (0 if which == 0 else 4)
        gi_post = (1 if which == 0 else 5)
        tag = "f%d%d" % (layer, which)
        with ExitStack() as es:
            sb = lambda n, sh, dt: es.enter_context(nc.sbuf_tensor(tag + n, sh, dt))
            ps = lambda n, sh, dt=F32: es.enter_context(nc.psum_tensor(tag + n, sh, dt))
            if pre is None:
                win, wout, start_w = self.ffn_weights(es, layer, which)
                start_w()
            else:
                win, wout = pre
            gpost = sb("gpost", [128, D], F32)
            xt = [sb("xt%d" % i, [128, 2, D], F32) for i in range(2)]
            hb = sb("hb", [128, 2, D], BF16)
            hT = sb("hT", [128, 8, TB], BF16)
            aT = sb("aT", [128, NFC, TB], BF16)
            sg = [sb("sg%d" % i, [128, TB], BF16) for i in range(2)]
            tmp = sb("tmp", [128, D], F32)
            tp = [ps("tp%d" % i, [128, 4, TB], BF16) for i in range(2)]
            gu = [ps("gu%d" % i, [128, 2, TB]) for i in range(2)]
            ob = [ps("ob%d" % i, [128, 2, 512]) for i in range(2)]
            K = lambda n: tag + n

            kb.dma("sp", gpost[:], self.norm_g[layer, gi_post, :].partition_broadcast(128), w=["gpost"], sem=K("gp"))

            def load(b):
                s = b % 2
                kb.dma("sp", xt[s][:], src[b * TB:(b + 1) * TB, :].rearrange("(j p) d -> p j d", p=128),
                       r=[("xd", b)], w=[K("xt%d" % s)], sem=K("xt%d" % s))

            def A1(b):
                s = b % 2
                for j in range(2):
                    c = self.rstd(xt[s][:, j, :], K("xt%d" % s), D)
                    kb.op("act", lambda e, j=j, c=c, s=s: e.activation(
                        out=hb[:, j, :], in_=xt[s][:, j, :], func=AF.Copy, scale=self.stt[:, c:c + 1]),
                        r=[K("xt%d" % s), ("st", c)], w=[K("hb")])

            def A2(b):
                for g in range(2):
                    t = tp[g]
                    for kci in range(4):
                        kc = g * 4 + kci
                        for j in range(2):
                            kb.op("pe", lambda e, t=t, kci=kci, kc=kc, j=j: e.transpose(
                                out=t[:, kci, j * 128:(j + 1) * 128], in_=hb[:, j, kc * 128:(kc + 1) * 128],
                                identity=self.ident[:]), r=[K("hb"), "ident"], w=[K("tp%d" % g)])
                    gsl = self.gT[:, gi_pre, g * 4:(g + 1) * 4].unsqueeze(2).to_broadcast([128, 4, TB])
                    kb.op("dve", lambda e, t=t, g=g, gsl=gsl: e.tensor_tensor(
                        out=hT[:, g * 4:(g + 1) * 4, :], in0=t[:], in1=gsl, op=ALU.mult),
                        r=[K("tp%d" % g), "gT"], w=[K("hT")])

            def B(b):
                for fc in range(NFC):
                    bank = gu[fc % 2]
                    bk = K("gu%d" % (fc % 2))
                    for half in range(2):
                        c0 = half * DFF + fc * 128
                        for kc in range(8):
                            kb.op("pe", lambda e, bank=bank, half=half, c0=c0, kc=kc: e.matmul(
                                bank[:, half, :], lhsT=win[:, kc, c0:c0 + 128], rhs=hT[:, kc, :],
                                start=(kc == 0), stop=(kc == 7)), r=[K("win"), K("hT")], w=[bk])
                    sgt = sg[fc % 2]
                    sk = K("sg%d" % (fc % 2))
                    kb.op("act", lambda e, bank=bank, sgt=sgt: e.activation(out=sgt[:], in_=bank[:, 0, :], func=AF.Silu),
                          r=[bk], w=[sk])
                    kb.op("dve", lambda e, bank=bank, sgt=sgt, fc=fc: e.tensor_tensor(
                        out=aT[:, fc, :], in0=bank[:, 1, :], in1=sgt[:], op=ALU.mult),
                        r=[bk, sk], w=[K("aT")])

            def C(b):
                s = b % 2
                for j in range(2):
                    o = ob[j]
                    ok = K("ob%d" % j)
                    for half in range(2):
                        for fc in range(NFC):
                            kb.op("pe", lambda e, o=o, half=half, fc=fc, j=j: e.matmul(
                                o[:, half, :], lhsT=aT[:, fc, j * 128:(j + 1) * 128],
                                rhs=wout[:, fc, half * 512:(half + 1) * 512],
                                start=(fc == 0), stop=(fc == NFC - 1)), r=[K("aT"), K("wout")], w=[ok])
                    self.epilogue(o[:].rearrange("p a b -> p (a b)"), ok, xt[s][:, j, :], K("xt%d" % s), gpost, 0.5, tmp)
                kb.dma("sp", self.xd[b * TB:(b + 1) * TB, :].rearrange("(j p) d -> p j d", p=128), xt[s][:],
                       r=[K("xt%d" % s)], w=[("xd", b)], sem=K("xt%d" % s))

            load(0)
            A1(0)
            A2(0)
            for b in range(NB):
                if b + 1 < NB:
                    load(b + 1)
                B(b)
                if b + 1 < NB:
                    A1(b + 1)
                C(b)
                if b + 1 < NB:
                    A2(b + 1)
            kb.barrier()

    def prenorm_hb(self, xt_ap, xt_key, hb_ap, hb_key):
        kb = self.kb
        c = self.rstd(xt_ap, xt_key, D)
        kb.op("act", lambda e: e.activation(out=hb_ap, in_=xt_ap, func=AF.Copy, scale=self.stt[:, c:c + 1]),
              r=[xt_key, ("st", c)], w=[hb_key])

    def transposes(self, hb, hb_key, nj, nkc, hT, hT_key, col0, tp, tp_keys, gain_ap_fn, gain_key, layout8=False):
        kb = self.kb
        W = nj * 128
        for g in range((nkc + 3) // 4):
            t = tp[g % 2]
            tk = tp_keys[g % 2]
            n4 = min(4, nkc - g * 4)
            for kci in range(n4):
                kc = g * 4 + kci
                for j in range(nj):
                    kb.op("pe", lambda e, t=t, kci=kci, kc=kc, j=j: e.transpose(
                        out=t[:, kci, j * 128:(j + 1) * 128], in_=hb[:, j, kc * 128:(kc + 1) * 128],
                        identity=self.ident[:]), r=[hb_key, "ident"], w=[tk])
            if layout8:
                dst8 = hT[:, g * 4:g * 4 + n4, :, col0 // 8:(col0 + W) // 8]
                src8 = t[:, 0:n4, 0:W].rearrange("p k (b s) -> p k s b", s=8)
                gsl8 = gain_ap_fn(g * 4, n4).unsqueeze(2).unsqueeze(3).to_broadcast([128, n4, 8, W // 8])
                kb.op("dve", lambda e, dst8=dst8, src8=src8, gsl8=gsl8: e.tensor_tensor(
                    out=dst8, in0=src8, in1=gsl8, op=ALU.mult), r=[tk, gain_key], w=[hT_key])
                continue
            dst = hT[:, g * 4:g * 4 + n4, col0:col0 + W]
            if gain_ap_fn is None:
                kb.op("act", lambda e, t=t, dst=dst, n4=n4: e.copy(out=dst, in_=t[:, 0:n4, 0:W]), r=[tk], w=[hT_key])
            else:
                gsl = gain_ap_fn(g * 4, n4).unsqueeze(2).to_broadcast([128, n4, W])
                kb.op("dve", lambda e, t=t, dst=dst, gsl=gsl, n4=n4: e.tensor_tensor(
                    out=dst, in0=t[:, 0:n4, 0:W], in1=gsl, op=ALU.mult), r=[tk, gain_key], w=[hT_key])

    def phase_ple(self, layer, src, start_w=None):
        kb = self.kb
        nc = self.nc
        TB = 256 if start_w is not None else 512
        NJ = TB // 128
        NB = DBG_NB or (S // TB)
        PIPE = int(os.environ.get('KPLE_PIPE', '0'))
        gi_pre = layer * 8 + 6
        tag = "p%d" % layer
        with ExitStack() as es:
            sb = lambda n, sh, dt: es.enter_context(nc.sbuf_tensor(tag + n, sh, dt))
            ps = lambda n, sh, dt=F32: es.enter_context(nc.psum_tensor(tag + n, sh, dt))
            wg = sb("wg", [128, 8, D], BF16)
            wp = sb("wp", [128, 2, D], BF16)
            gpost = sb("gpost", [128, D], F32)
            xt = [sb("xt%d" % i, [128, NJ, D], F32) for i in range(2)]
            pt = [sb("pt%d" % i, [128, NJ, PLE], F32) for i in range(2)]
            hbs = [sb("hb%d" % i, [128, NJ, D], BF16) for i in range(1 + PIPE)]
            pbs = [sb("pb%d" % i, [128, NJ, PLE], BF16) for i in range(1 + PIPE)]
            hTs = [sb("hT%d" % i, [128, 8, TB], BF16) for i in range(1 + PIPE)]
            pTs = [sb("pT%d" % i, [128, 2, TB], BF16) for i in range(1 + PIPE)]
            sgm = sb("sgm", [128, D], F32)
            vt = sb("vt", [128, D], F32)
            tmp = sb("tmp", [128, D], F32)
            tp = [ps("tp%d" % i, [128, 4, TB], BF16) for i in range(2)]
            gate = ps("gate", [128, 2, 512])
            pe_ = ps("pe", [128, 2, 512])
            K = lambda n: tag + n
            self.load_w(wg, K("wg"), self.ple_w_gate[layer].rearrange("(kc p) n -> p kc n", p=128), 2, D, K("wg"))
            self.load_w(wp, K("wp"), self.ple_w_proj[layer].rearrange("(kc p) n -> p kc n", p=128), 1, D, K("wp"))
            kb.dma("sp", gpost[:], self.norm_g[layer, 7, :].partition_broadcast(128), w=["gpost"], sem=K("gp"))
            if start_w is not None:
                start_w()

            def load(b):
                s = b % 2
                kb.dma("sp", xt[s][:], src[b * TB:(b + 1) * TB, :].rearrange("(j p) d -> p j d", p=128),
                       r=[("xd", b)], w=[K("xt%d" % s)], sem=K("xt%d" % s))
                kb.dma("sp", pt[s][:], self.p_in[layer, b * TB:(b + 1) * TB, :].rearrange("(j p) d -> p j d", p=128),
                       w=[K("pt%d" % s)], sem=K("pt%d" % s))

            def A(b):
                s = b % 2
                s2 = s if PIPE else 0
                hb, pb, hT, pT = hbs[s2], pbs[s2], hTs[s2], pTs[s2]
                for j in range(NJ):
                    self.prenorm_hb(xt[s][:, j, :], K("xt%d" % s), hb[:, j, :], K("hb%d" % s2))
                kb.op("dve", lambda e, s=s, pb=pb: e.tensor_copy(out=pb[:], in_=pt[s][:]), r=[K("pt%d" % s)], w=[K("pb%d" % s2)])
                self.transposes(hb, K("hb%d" % s2), NJ, 8, hT, K("hT%d" % s2), 0, tp, [K("tp0"), K("tp1")],
                                lambda k0, n: self.gT[:, gi_pre, k0:k0 + n], "gT")
                self.transposes(pb, K("pb%d" % s2), NJ, 2, pT, K("pT%d" % s2), 0, tp, [K("tp0"), K("tp1")], None, None)

            def C(b):
                s = b % 2
                s2 = s if PIPE else 0
                hT, pT = hTs[s2], pTs[s2]
                for j in range(NJ):
                    for half in range(2):
                        for kc in range(8):
                            kb.op("pe", lambda e, half=half, kc=kc, j=j, hT=hT: e.matmul(
                                gate[:, half, :], lhsT=hT[:, kc, j * 128:(j + 1) * 128],
                                rhs=wg[:, kc, half * 512:(half + 1) * 512], start=(kc == 0), stop=(kc == 7)),
                                r=[K("hT%d" % s2), K("wg")], w=[K("gate")])
                    for half in range(2):
                        for kc in range(2):
                            kb.op("pe", lambda e, half=half, kc=kc, j=j, pT=pT: e.matmul(
                                pe_[:, half, :], lhsT=pT[:, kc, j * 128:(j + 1) * 128],
                                rhs=wp[:, kc, half * 512:(half + 1) * 512], start=(kc == 0), stop=(kc == 1)),
                                r=[K("pT%d" % s2), K("wp")], w=[K("pe")])
                    kb.op("act", lambda e: e.activation(out=sgm[:], in_=gate[:].rearrange("p a b -> p (a b)"),
                                                        func=AF.Sigmoid), r=[K("gate")], w=[K("sgm")])
                    kb.op("dve", lambda e: e.tensor_tensor(out=vt[:], in0=pe_[:].rearrange("p a b -> p (a b)"),
                                                           in1=sgm[:], op=ALU.mult), r=[K("pe"), K("sgm")], w=[K("vt")])
                    self.epilogue(vt[:], K("vt"), xt[s][:, j, :], K("xt%d" % s), gpost, 1.0, tmp)
                kb.dma("sp", self.xd[b * TB:(b + 1) * TB, :].rearrange("(j p) d -> p j d", p=128), xt[s][:],
                       r=[K("xt%d" % s)], w=[("xd", b)], sem=K("xt%d" % s))

            load(0)
            if PIPE:
                A(0)
            for b in range(NB):
                if b + 1 < NB:
                    load(b + 1)
                if PIPE:
                    if b + 1 < NB:
                        A(b + 1)
                else:
                    A(b)
                C(b)
            kb.barrier()

    def setup_attn_consts(self):
        kb = self.kb
        nc = self.nc
        with ExitStack() as es:
            sb = lambda n, sh, dt: es.enter_context(nc.sbuf_tensor("ac%d" % self.kb.q["pe"].count + n, sh, dt))
            relb = sb("relb", [32, 16], F32)
            oh = sb("oh", [32, 128], F32)
            wt = sb("wt", [16, 384], F32)
            pT_ = es.enter_context(nc.psum_tensor("acps%d" % self.kb.q["pe"].count, [16, 128], F32))
            kb.dma("sp", relb[:], self.rel_bias[:, :], w=["relb"], sem="ac0")
            kb.dma("sp", oh[:], self.oh_in[:, :], w=["oh"], sem="ac0b")
            kb.op("pe", lambda e: e.matmul(pT_[:], lhsT=relb[:], rhs=oh[:], start=True, stop=True),
                  r=["relb", "oh"], w=["acps"])
            kb.op("dve", lambda e: e.memset(wt[:, 0:128], -30000.0), w=["wt"])
            kb.op("dve", lambda e: e.memset(wt[:, 256:384], 0.0), w=["wt"])
            kb.op("dve", lambda e: e.tensor_copy(out=wt[:, 128:256], in_=pT_[:]), r=["acps"], w=["wt"])
            kb.dma("sp", self.wscr[:, :], wt[:], r=["wt"], w=["wscr"], sem="ac1")
            for col in range(16):
                for which in range(2):
                    off = col * 384 + (1 if which == 0 else 129)
                    src = bass.AP(tensor=self.wscr_h, offset=off, ap=[[1, 128], [1, 128]])
                    kb.dma("pool", self.brev[:, col, which, :], src, r=["wscr"], w=["brev"], sem="ac2")
            kb.barrier()

    def phase_attn(self, layer, src):
        kb = self.kb
        nc = self.nc
        j_att = layer // 2
        lam_init = 0.8 - 0.6 * float(np.exp(-0.3 * layer))
        TB = 256
        NBt = S // TB
        NQB = DBG_NB or 8
        NH = int(os.environ.get("KDBG_NH", "8"))
        gi_pre = layer * 8 + 2
        tag = "a%d" % layer
        with ExitStack() as es:
            sb = lambda n, sh, dt: es.enter_context(nc.sbuf_tensor(tag + n, sh, dt))
            ps = lambda n, sh, dt=F32: es.enter_context(nc.psum_tensor(tag + n, sh, dt))
            K = lambda n: tag + n
            self.brev = sb("brev", [128, 16, 2, 128], BF16)
            self.setup_attn_consts()
            hT = sb("hT", [128, 8, S], BF16)
            xt = [sb("xt%d" % i, [128, 2, D], F32) for i in range(2)]
            hbs = [sb("hb%d" % i, [128, 2, D], BF16) for i in range(2)]
            wq = [sb("wq%d" % i, [128, 8, 128], BF16) for i in range(2)]
            wk = [sb("wk%d" % i, [128, 8, 128], BF16) for i in range(2)]
            wv = [sb("wv%d" % i, [128, 8, 128], BF16) for i in range(2)]
            QT = [sb("QT%d" % m, [64, S], BF16) for m in range(2)]
            KT = [sb("KT%d" % m, [64, S], BF16) for m in range(2)]
            V1 = sb("V1", [128, 32, 130], BF16)
            ET = [sb("ET%d" % i, [128, 512], BF16) for i in range(4)]
            ot = [sb("ot%d" % i, [128, 4, 128], F32) for i in range(2)]
            Osb = [sb("Osb%d" % i, [128, 4, 129], F32) for i in range(2)]
            o1 = sb("o1", [128, 4, 128], F32)
            o2 = sb("o2", [128, 4, 128], F32)
            fst = sb("fst", [128, 16], F32)
            lamt = sb("lamt", [128, 256], F32)
            lamp = sb("lamp", [128, 128], F32)
            lams = sb("lams", [128, 8], F32)
            es_tp = ExitStack()
            tp = [es_tp.enter_context(nc.psum_tensor(tag + "tp%d" % i, [128, 4, TB], BF16)) for i in range(2)]

            kb.dma("sp", lamt[:], self.attn_lam[j_att].rearrange("a b -> (a b)").partition_broadcast(128),
                   w=[K("lamt")], sem=K("lam"))
            lv = lamt[:].rearrange("p (a b c) -> p a b c", a=2, b=2)
            kb.op("dve", lambda e: e.tensor_tensor(out=lamp[:].rearrange("p (a c) -> p a c", a=2), in0=lv[:, :, 0, :],
                                                   in1=lv[:, :, 1, :], op=ALU.mult), r=[K("lamt")], w=[K("lamp")])
            kb.op("dve", lambda e: e.tensor_reduce(out=lams[:, 0:2], in_=lamp[:].rearrange("p (a c) -> p a c", a=2),
                                                   axis=AX.X, op=ALU.add), r=[K("lamp")], w=[K("lams")])
            kb.op("act", lambda e: e.activation(out=lams[:, 2:4], in_=lams[:, 0:2], func=AF.Exp),
                  r=[K("lams")], w=[K("lams2")])
            kb.op("dve", lambda e: e.tensor_tensor(out=lams[:, 4:5], in0=lams[:, 3:4], in1=lams[:, 2:3],
                                                   op=ALU.subtract), r=[K("lams2")], w=[K("lams3")])
            kb.op("dve", lambda e: e.tensor_scalar(out=lams[:, 5:6], in0=lams[:, 4:5], scalar1=-lam_init, scalar2=None,
                                                   op0=ALU.add), r=[K("lams3")], w=[K("neglam")])
            kb.op("dve", lambda e: e.memset(V1[:, :, 128:130], 1.0), w=[K("V1")])

            def load(b):
                s = b % 2
                kb.dma("sp", xt[s][:], src[b * TB:(b + 1) * TB, :].rearrange("(j p) d -> p j d", p=128),
                       r=[("xd", b)], w=[K("xt%d" % s)], sem=K("xt%d" % s))
            load(0)
            for b in range(NBt):
                if b + 1 < NBt:
                    load(b + 1)
                s = b % 2
                hb = hbs[s]
                for j in range(2):
                    self.prenorm_hb(xt[s][:, j, :], K("xt%d" % s), hb[:, j, :], K("hb%d" % s))
                self.transposes(hb, K("hb%d" % s), 2, 8, hT, K("hT"), b * TB, tp, [K("tp0"), K("tp1")],
                                lambda k0, n: self.gT[:, gi_pre, k0:k0 + n], "gT")

            kb.barrier()
            es_tp.close()
            NS = 4
            sT = [ps("sT%d" % i, [128, 512]) for i in range(NS)]
            Oa = [ps("O%d" % m, [128, 4, 256]) for m in range(2)]
            wsrc = self.attn_w_qkv[j_att].rearrange("(kc p) n -> p kc n", p=128)

            def loadw(h):
                s = h % 2
                kb.dma("pool", wq[s][:], wsrc[:, :, h * 128:(h + 1) * 128], w=[K("wq%d" % s)], sem=K("wq%d" % s))
                kb.dma("pool", wk[s][:], wsrc[:, :, D + h * 128:D + (h + 1) * 128], w=[K("wk%d" % s)], sem=K("wk%d" % s))
                kb.dma("pool", wv[s][:], wsrc[:, :, 2 * D + h * 128:2 * D + (h + 1) * 128], w=[K("wv%d" % s)], sem=K("wv%d" % s))

            loadw(0)
            cnt = 0
            for h in range(NH):
                if h + 1 < NH:
                    loadw(h + 1)
                s = h % 2
                for m in range(2):
                    for (wt_, wkey, dst, dkey, scale) in ((wq[s], K("wq%d" % s), QT[m], K("QT%d" % m), 0.125),
                                                        (wk[s], K("wk%d" % s), KT[m], K("KT%d" % m), 1.0)):
                        for tb in range(8):
                            bank = sT[cnt % NS]
                            bk = K("sT%d" % (cnt % NS))
                            cnt += 1
                            for kc in range(8):
                                kb.op("pe", lambda e, bank=bank, wt_=wt_, kc=kc, m=m, tb=tb: e.matmul(
                                    bank[0:64, :], lhsT=wt_[:, kc, m * 64:(m + 1) * 64], rhs=hT[:, kc, tb * 512:(tb + 1) * 512],
                                    start=(kc == 0), stop=(kc == 7)), r=[wkey, K("hT")], w=[bk])
                            kb.op("act", lambda e, bank=bank, dst=dst, tb=tb, scale=scale: e.activation(
                                out=dst[:, tb * 512:(tb + 1) * 512], in_=bank[0:64, :], func=AF.Copy, scale=scale),
                                r=[bk], w=[dkey])
                for t4 in range(8):
                    bank = sT[cnt % NS]
                    bk = K("sT%d" % (cnt % NS))
                    cnt += 1
                    for ti in range(4):
                        tt = t4 * 4 + ti
                        for kc in range(8):
                            kb.op("pe", lambda e, bank=bank, ti=ti, tt=tt, kc=kc, wvs=wv[s]: e.matmul(
                                bank[:, ti * 128:(ti + 1) * 128], lhsT=hT[:, kc, tt * 128:(tt + 1) * 128], rhs=wvs[:, kc, :],
                                start=(kc == 0), stop=(kc == 7)), r=[K("wv%d" % s), K("hT")], w=[bk])
                    kb.op("dve", lambda e, bank=bank, t4=t4: e.tensor_copy(
                        out=V1[:, t4 * 4:(t4 + 1) * 4, 0:128], in_=bank[:].rearrange("p (a b) -> p a b", a=4)),
                        r=[bk], w=[K("V1")])
                st = self.stt

                def emit_S(step):
                    nonlocal cnt
                    qb, m, kt = step["qb"], step["m"], step["kt"]
                    col = h * 2 + m
                    i = kt - 4 * qb
                    j0 = max(i, 0)
                    bank = sT[cnt % NS]
                    bk = K("sT%d" % (cnt % NS))
                    et = ET[cnt % 4]
                    ek = K("ET%d" % (cnt % 4))
                    cnt += 1
                    step["et"], step["ek"], step["j0"] = et, ek, j0
                    ksl = KT[m][:, kt * 128:(kt + 1) * 128]
                    j = j0
                    while j < 4:
                        delta = j - i
                        if delta in (0, 1):
                            c0, c1 = j * 128, (j + 1) * 128
                            kb.op("pe", lambda e, bank=bank, ksl=ksl, m=m, c0=c0, c1=c1, qb=qb: e.matmul(
                                bank[:, c0:c1], lhsT=ksl, rhs=QT[m][:, qb * 512 + c0:qb * 512 + c1],
                                start=True, stop=False), r=[K("KT%d" % m), K("QT%d" % m)], w=[bk])
                            kb.op("pe", lambda e, bank=bank, c0=c0, c1=c1, col=col, delta=delta: e.matmul(
                                bank[:, c0:c1], lhsT=self.jflip[:], rhs=self.brev[:, col, delta, :],
                                start=False, stop=True), r=["jflip", "brev"], w=[bk])
                            j += 1
                        else:
                            c0, c1 = j * 128, 512
                            kb.op("pe", lambda e, bank=bank, ksl=ksl, m=m, c0=c0, c1=c1, qb=qb: e.matmul(
                                bank[:, c0:c1], lhsT=ksl, rhs=QT[m][:, qb * 512 + c0:qb * 512 + c1],
                                start=True, stop=True), r=[K("KT%d" % m), K("QT%d" % m)], w=[bk])
                            j = 4
                    kb.op("act", lambda e, bank=bank, et=et, j0=j0: e.activation(
                        out=et[:, j0 * 128:512], in_=bank[:, j0 * 128:512], func=AF.Exp), r=[bk], w=[ek])

                def emit_PV(step):
                    qb, m, kt = step["qb"], step["m"], step["kt"]
                    et, ek, j0 = step["et"], step["ek"], step["j0"]
                    O = Oa[m]
                    Ok = K("O%d" % m)
                    for j in range(j0, 4):
                        kb.op("pe", lambda e, O=O, et=et, j=j, kt=kt: e.matmul(
                            O[:, j, 0:129], lhsT=et[:, j * 128:(j + 1) * 128], rhs=V1[:, kt, 0:129],
                            start=(kt == 0 and j in (0, 2)), stop=False, skip_group_check=True),
                            r=[ek, K("V1")], w=[Ok])
                    if kt == 4 * qb + 3:
                        kb.op("act", lambda e, O=O, m=m: e.copy(out=Osb[m][:], in_=O[:, :, 0:129]),
                              r=[Ok], w=[K("Osb%d" % m)])
                        if m == 1:
                            finalize(qb)

                def finalize(qb):
                    osl = ot[(h * 8 + qb) % 2]
                    okey = K("ot%d" % ((h * 8 + qb) % 2))
                    bc = lambda ap: ap.to_broadcast([128, 4, 128])
                    kb.op("dve", lambda e: e.reciprocal(out=fst[:, 0:4], in_=Osb[0][:, :, 128]), r=[K("Osb0")], w=[K("f0")])
                    kb.op("dve", lambda e: e.reciprocal(out=fst[:, 4:8], in_=Osb[1][:, :, 128]), r=[K("Osb1")], w=[K("f1")])
                    kb.op("dve", lambda e: e.tensor_scalar(out=fst[:, 4:8], in0=fst[:, 4:8], scalar1=lams[:, 5:6], scalar2=None,
                                                           op0=ALU.mult), r=[K("f1"), K("neglam")], w=[K("f1")])
                    kb.op("dve", lambda e: e.tensor_tensor(out=o1[:], in0=Osb[0][:, :, 0:128], in1=bc(fst[:, 0:4].unsqueeze(2)),
                                                           op=ALU.mult), r=[K("Osb0"), K("f0")], w=[K("o1")])
                    kb.op("dve", lambda e: e.tensor_tensor(out=o2[:], in0=Osb[1][:, :, 0:128], in1=bc(fst[:, 4:8].unsqueeze(2)),
                                                           op=ALU.mult), r=[K("Osb1"), K("f1")], w=[K("o2")])
                    kb.op("dve", lambda e: e.tensor_tensor(out=o1[:], in0=o1[:], in1=o2[:], op=ALU.add),
                          r=[K("o1"), K("o2")], w=[K("o1")])
                    kb.op("pool", lambda e: e.tensor_tensor(out=o2[:], in0=o1[:], in1=o1[:], op=ALU.mult),
                          r=[K("o1"), K("o2")], w=[K("o2")])
                    kb.op("dve", lambda e: e.tensor_reduce(out=fst[:, 8:12], in_=o2[:], axis=AX.X, op=ALU.add),
                          r=[K("o2")], w=[K("f2")])
                    kb.op("act", lambda e: e.activation(out=fst[:, 12:16], in_=fst[:, 8:12], func=AF.Sqrt,
                                                        scale=1.0 / 128, bias=self.epsb[:, 0:1]),
                          r=[K("f2"), "epsb"], w=[K("f3")])
                    kb.op("dve", lambda e: e.reciprocal(out=fst[:, 8:12], in_=fst[:, 12:16]), r=[K("f3"), K("f2")], w=[K("f2")])
                    kb.op("dve", lambda e, osl=osl: e.tensor_tensor(out=osl[:], in0=o1[:], in1=bc(fst[:, 8:12].unsqueeze(2)),
                                                                    op=ALU.mult), r=[K("o1"), K("f2")], w=[okey])
                    dst = self.od[qb * 512:(qb + 1) * 512, h * 128:(h + 1) * 128].rearrange("(j p) v -> p j v", p=128)
                    kb.dma("sp", dst, osl[:], r=[okey], w=[("od", qb)], sem=okey)

                steps = [dict(qb=qb, m=m, kt=kt) for qb in range(NQB) for m in range(2) for kt in range(4 * qb + 4)]
                LAG = int(os.environ.get("KLAG", "3"))
                pend = []
                for stp in steps:
                    emit_S(stp)
                    pend.append(stp)
                    if len(pend) > LAG:
                        emit_PV(pend.pop(0))
                while pend:
                    emit_PV(pend.pop(0))
            kb.barrier()

    def phase_attn_out(self, layer, start_w=None):
        kb = self.kb
        nc = self.nc
        j_att = layer // 2
        lam_init = 0.8 - 0.6 * float(np.exp(-0.3 * layer))
        TB = 256
        NB = DBG_NB * 2 if DBG_NB else (S // TB)
        tag = "ao%d" % layer
        with ExitStack() as es:
            sb = lambda n, sh, dt: es.enter_context(nc.sbuf_tensor(tag + n, sh, dt))
            ps = lambda n, sh, dt=F32: es.enter_context(nc.psum_tensor(tag + n, sh, dt))
            K = lambda n: tag + n
            wo = sb("wo", [128, 8, D], BF16)
            gpost = sb("gpost", [128, D], F32)
            sgn = sb("sgn", [128, 2], F32)
            xt = [sb("xt%d" % i, [128, 2, D], F32) for i in range(2)]
            ol = [sb("ol%d" % i, [128, 2, D], F32) for i in range(2)]
            hbs = [sb("hb%d" % i, [128, 2, D], BF16) for i in range(1)]
            hTs = [sb("hT%d" % i, [128, 8, TB], BF16) for i in range(1)]
            tmp = sb("tmp", [128, D], F32)
            tp = [ps("tp%d" % i, [128, 4, TB], BF16) for i in range(2)]
            ob = [ps("ob%d" % i, [128, 2, 512]) for i in range(2)]
            self.load_w(wo, K("wo"), self.attn_w_o[j_att].rearrange("(kc p) n -> p kc n", p=128), 2, D, K("wo"))
            kb.dma("sp", gpost[:], self.norm_g[layer, 3, :].partition_broadcast(128), w=["gpost"], sem=K("gp"))
            kb.dma("sp", sgn[:, 0:1], self.attn_subln_g[j_att].rearrange("(p o) -> p o", o=1), w=[K("sgn")], sem=K("sgn"))
            kb.op("dve", lambda e: e.tensor_scalar(out=sgn[:, 1:2], in0=sgn[:, 0:1], scalar1=1.0 - lam_init, scalar2=None,
                                                   op0=ALU.mult), r=[K("sgn")], w=[K("sgn2")])

            def load(b):
                s = b % 2
                kb.dma("sp", xt[s][:], self.xd[b * TB:(b + 1) * TB, :].rearrange("(j p) d -> p j d", p=128),
                       r=[("xd", b)], w=[K("xt%d" % s)], sem=K("xt%d" % s))
                kb.dma("sp", ol[s][:], self.od[b * TB:(b + 1) * TB, :].rearrange("(j p) d -> p j d", p=128),
                       r=[("od", b // 2)], w=[K("ol%d" % s)], sem=K("ol%d" % s))

            def A(b):
                s = b % 2
                hb, hT = hbs[0], hTs[0]
                kb.op("act", lambda e, s=s, hb=hb: e.copy(out=hb[:], in_=ol[s][:]), r=[K("ol%d" % s)], w=[K("hb0")])
                self.transposes(hb, K("hb0"), 2, 8, hT, K("hT0"), 0, tp, [K("tp0"), K("tp1")],
                                lambda k0, n: sgn[:, 1:2].to_broadcast([128, n]), K("sgn2"))

            def C(b):
                s = b % 2
                hT = hTs[0]
                for j in range(2):
                    o = ob[j]
                    ok = K("ob%d" % j)
                    for half in range(2):
                        for kc in range(8):
                            kb.op("pe", lambda e, o=o, half=half, kc=kc, j=j, hT=hT: e.matmul(
                                o[:, half, :], lhsT=hT[:, kc, j * 128:(j + 1) * 128],
                                rhs=wo[:, kc, half * 512:(half + 1) * 512], start=(kc == 0), stop=(kc == 7)),
                                r=[K("hT0"), K("wo")], w=[ok])
                    self.epilogue(o[:].rearrange("p a b -> p (a b)"), ok, xt[s][:, j, :], K("xt%d" % s), gpost, 1.0, tmp)
                kb.dma("sp", self.xd[b * TB:(b + 1) * TB, :].rearrange("(j p) d -> p j d", p=128), xt[s][:],
                       r=[K("xt%d" % s)], w=[("xd", b)], sem=K("xt%d" % s))

            if start_w is not None:
                start_w()
            load(0)
            for b in range(NB):
                if b + 1 < NB:
                    load(b + 1)
                A(b)
                C(b)
            kb.barrier()


    def cmul(self, eng, out_re, out_im, a_re, a_im, b_re, b_im, t1, t2, keys_r, key_w):
        kb = self.kb
        tk = "cmt"
        kb.op(eng, lambda e: e.tensor_tensor(out=t1, in0=a_re, in1=b_re, op=ALU.mult), r=keys_r, w=[tk + "1"])
        kb.op(eng, lambda e: e.tensor_tensor(out=t2, in0=a_im, in1=b_im, op=ALU.mult), r=keys_r, w=[tk + "2"])
        kb.op(eng, lambda e: e.tensor_tensor(out=out_re, in0=t1, in1=t2, op=ALU.subtract), r=[tk + "1", tk + "2"], w=[key_w + "r"])
        kb.op(eng, lambda e: e.tensor_tensor(out=t1, in0=a_re, in1=b_im, op=ALU.mult), r=keys_r + [key_w + "r"], w=[tk + "1"])
        kb.op(eng, lambda e: e.tensor_tensor(out=t2, in0=a_im, in1=b_re, op=ALU.mult), r=keys_r + [key_w + "r"], w=[tk + "2"])
        kb.op(eng, lambda e: e.tensor_tensor(out=out_im, in0=t1, in1=t2, op=ALU.add), r=[tk + "1", tk + "2"], w=[key_w + "i"])

    def phase_ssm(self, layer, src):
        kb = self.kb
        nc = self.nc
        js = layer // 2
        TB = 256
        NBt = S // TB
        NGP = DBG_NB or 32
        gi_pre = layer * 8 + 2
        tag = "s%d" % layer
        PI = float(np.pi)
        with ExitStack() as es:
            sb = lambda n, sh, dt: es.enter_context(nc.sbuf_tensor(tag + n, sh, dt))
            ps = lambda n, sh, dt=F32: es.enter_context(nc.psum_tensor(tag + n, sh, dt))
            K = lambda n: tag + n
            WinT = sb("WinT", [128, 64, 2, 64], BF16)
            Vre = sb("Vre", [128, 32, 128], BF16)
            Vim = sb("Vim", [128, 32, 128], BF16)
            T0 = sb("T0", [128, 64, 128], BF16)
            ALr = sb("ALr", [128, 9, 32], F32)
            ALi = sb("ALi", [128, 9, 32], F32)
            ALn = sb("ALn", [128, 9, 32], F32)
            es_tp = ExitStack()
            tp = [es_tp.enter_context(nc.psum_tensor(tag + "tp%d" % i, [128, 4, TB], BF16)) for i in range(2)]
            pT0 = [es_tp.enter_context(nc.psum_tensor(tag + "pt0%d" % i, [128, 128], F32)) for i in range(2)]
            with ExitStack() as es2:
                sb2 = lambda n, sh, dt: es2.enter_context(nc.sbuf_tensor(tag + n, sh, dt))
                lr = sb2("lr", [128, 32], F32)
                li = sb2("li", [128, 32], F32)
                ldt = sb2("ldt", [128, 32], F32)
                br = sb2("br", [128, 32, 16], F32)
                bi = sb2("bi", [128, 32, 16], F32)
                cr = sb2("cr", [128, 32, 16], F32)
                ci = sb2("ci", [128, 32, 16], F32)
                dcol = sb2("dcol", [128, 64], F32)
                maskt = sb2("maskt", [128, 128], F32)
                identf = sb2("identf", [128, 128], F32)
                sm = sb2("sm", [128, 24, 32], F32)
                P_re = sb2("Pre", [128, 9, 32], F32)
                P_im = sb2("Pim", [128, 9, 32], F32)
                bbr = sb2("bbr", [128, 32, 16], F32)
                bbi = sb2("bbi", [128, 32, 16], F32)
                w1 = sb2("w1", [128, 32, 16], F32)
                w2 = sb2("w2", [128, 32, 16], F32)
                E_re = sb2("Ere", [128, 32, 8, 16], F32)
                E_im = sb2("Eim", [128, 32, 8, 16], F32)
                F_re = sb2("Fre", [128, 32, 8, 16], F32)
                F_im = sb2("Fim", [128, 32, 8, 16], F32)
                Eb_re = sb2("Ebre", [128, 32, 128], BF16)
                Eb_im = sb2("Ebim", [128, 32, 128], BF16)
                Gb_re = sb2("Gbre", [128, 32, 128], BF16)
                Gb_im = sb2("Gbim", [128, 32, 128], BF16)
                t0s = sb2("t0s", [128, 128], F32)

                def dram(tensor_ap, off, ap):
                    return bass.AP(tensor=tensor_ap.tensor, offset=off, ap=ap)
                kb.dma("sp", lr[:], dram(self.ssm_lam_re, js * 4096, [[1, 128], [128, 32]]), w=[K("lr")], sem=K("lr"), slow=True)
                kb.dma("sp", li[:], dram(self.ssm_lam_im, js * 4096, [[1, 128], [128, 32]]), w=[K("li")], sem=K("li"), slow=True)
                for g2 in range(2):
                    kb.dma("sp", ldt[64 * g2:64 * g2 + 64, :], dram(self.ssm_log_dt, js * 64 + g2, [[0, 64], [2, 32]]),
                           w=[K("ldt")], sem=K("ldt"), slow=True)
                kb.dma("sp", br[:], dram(self.ssm_b_re, js * 65536, [[16, 128], [2048, 32], [1, 16]]), w=[K("br")], sem=K("br"))
                kb.dma("sp", bi[:], dram(self.ssm_b_im, js * 65536, [[16, 128], [2048, 32], [1, 16]]), w=[K("bi")], sem=K("bi"))
                for g2 in range(2):
                    for h_ in range(16):
                        off = js * 65536 + g2 * 1024 + h_ * 64
                        kb.dma("sp", cr[64 * g2:64 * g2 + 64, :, h_], dram(self.ssm_c_re, off, [[1, 64], [2048, 32]]),
                               w=[K("cr")], sem=K("cr"), slow=True)
                        kb.dma("sp", ci[64 * g2:64 * g2 + 64, :, h_], dram(self.ssm_c_im, off, [[1, 64], [2048, 32]]),
                               w=[K("ci")], sem=K("ci"), slow=True)
                for s_ in range(8):
                    kb.dma("sp", dcol[16 * s_:16 * s_ + 16, :], dram(self.ssm_d, js * 1024, [[1, 16], [16, 64]]),
                           w=[K("dcol")], sem=K("dcol"), slow=True)
                kb.dma("sp", maskt[:], self.mask_in[:, :], w=[K("maskt")], sem=K("maskt"))
                kb.dma("sp", identf[:], self.ident_in[:, :], w=[K("identf")], sem=K("identf"))

                V = lambda i: sm[:, i, :]
                T = lambda eng, fn, r, w: kb.op(eng, fn, r=r, w=w)
                tt = lambda out, a, b, op, r, w, eng="dve": kb.op(eng, lambda e: e.tensor_tensor(out=out, in0=a, in1=b, op=op), r=r, w=w)
                ts = lambda out, a, s1, s2, op0, op1, r, w: kb.op("dve", lambda e: e.tensor_scalar(
                    out=out, in0=a, scalar1=s1, scalar2=s2, op0=op0, op1=op1), r=r, w=w)
                kb.op("act", lambda e: e.activation(out=V(0), in_=ldt[:], func=AF.Exp), r=[K("ldt")], w=[K("dt")])
                tt(V(1), lr[:], V(0), ALU.mult, [K("lr"), K("dt")], [K("ar")])
                tt(V(2), li[:], V(0), ALU.mult, [K("li"), K("dt")], [K("ai")])
                kb.op("act", lambda e: e.activation(out=V(3), in_=V(1), func=AF.Exp), r=[K("ar")], w=[K("mag")])
                kb.op("dve", lambda e: e.tensor_copy(out=V(5), in_=V(2)), r=[K("ai")], w=[K("rr")])
                for m_ in range(1, 7):
                    thr = (2 * m_ - 1) * PI
                    kb.op("dve", lambda e, thr=thr: e.tensor_scalar(out=V(4), in0=V(2), scalar1=thr, scalar2=-2 * PI,
                                                                    op0=ALU.is_ge, op1=ALU.mult), r=[K("ai"), K("rr")], w=[K("rm")])
                    kb.op("dve", lambda e: e.tensor_tensor(out=V(5), in0=V(5), in1=V(4), op=ALU.add),
                          r=[K("rm"), K("rr")], w=[K("rr")])
                kb.op("act", lambda e: e.activation(out=V(6), in_=V(5), func=AF.Sin), r=[K("rr")], w=[K("sin")])
                kb.op("act", lambda e: e.activation(out=V(7), in_=V(5), func=AF.Sin, scale=0.5), r=[K("rr")], w=[K("sinh")])
                tt(V(8), V(7), V(7), ALU.mult, [K("sinh")], [K("sh2")])
                ts(V(9), V(8), -2.0, 1.0, ALU.mult, ALU.add, [K("sh2")], [K("cos")])
                kb.op("dve", lambda e: e.memset(P_re[:, 0, :], 1.0), w=[K("P0r")])
                kb.op("dve", lambda e: e.memset(P_im[:, 0, :], 0.0), w=[K("P0i")])
                tt(P_re[:, 1, :], V(3), V(9), ALU.mult, [K("mag"), K("cos")], [K("P1r")])
                tt(P_im[:, 1, :], V(3), V(6), ALU.mult, [K("mag"), K("sin")], [K("P1i")])
                for k in range(1, 8):
                    self.cmul("dve", P_re[:, k + 1, :], P_im[:, k + 1, :], P_re[:, k, :], P_im[:, k, :],
                              P_re[:, 1, :], P_im[:, 1, :], V(10), V(11),
                              [K("P%dr" % k), K("P%di" % k), K("P1r"), K("P1i")], K("P%d" % (k + 1)))
                kb.op("dve", lambda e: e.tensor_copy(out=ALr[:, 0, :], in_=P_re[:, 8, :]), r=[K("P8r")], w=[K("AL0r")])
                kb.op("dve", lambda e: e.tensor_copy(out=ALi[:, 0, :], in_=P_im[:, 8, :]), r=[K("P8i")], w=[K("AL0i")])
                for l in range(8):
                    self.cmul("dve", ALr[:, l + 1, :], ALi[:, l + 1, :], ALr[:, l, :], ALi[:, l, :], ALr[:, l, :], ALi[:, l, :],
                              V(10), V(11), [K("AL%dr" % l), K("AL%di" % l)], K("AL%d" % (l + 1)))
                kb.op("dve", lambda e: e.tensor_scalar(out=ALn[:], in0=ALi[:], scalar1=-1.0, scalar2=None, op0=ALU.mult),
                      r=[K("AL%di" % l) for l in range(9)], w=[K("ALn")])
                allAL = [K("AL%dr" % l) for l in range(9)] + [K("AL%di" % l) for l in range(9)] + [K("ALn")]
                ts(V(12), P_re[:, 1, :], -1.0, None, ALU.add, None, [K("P1r")], [K("am1")]) if False else kb.op(
                    "dve", lambda e: e.tensor_scalar(out=V(12), in0=P_re[:, 1, :], scalar1=-1.0, scalar2=None, op0=ALU.add),
                    r=[K("P1r")], w=[K("am1")])
                tt(V(13), lr[:], lr[:], ALU.mult, [K("lr")], [K("d2a")])
                tt(V(14), li[:], li[:], ALU.mult, [K("li")], [K("d2b")])
                tt(V(13), V(13), V(14), ALU.add, [K("d2a"), K("d2b")], [K("d2")])
                kb.op("dve", lambda e: e.reciprocal(out=V(15), in_=V(13)), r=[K("d2")], w=[K("inv")])
                tt(V(16), V(12), lr[:], ALU.mult, [K("am1"), K("lr")], [K("n1")])
                tt(V(17), P_im[:, 1, :], li[:], ALU.mult, [K("P1i"), K("li")], [K("n2")])
                tt(V(16), V(16), V(17), ALU.add, [K("n1"), K("n2")], [K("n3")])
                tt(V(18), V(16), V(15), ALU.mult, [K("n3"), K("inv")], [K("c1r")])
                tt(V(19), P_im[:, 1, :], lr[:], ALU.mult, [K("P1i"), K("lr")], [K("m1")])
                tt(V(20), V(12), li[:], ALU.mult, [K("am1"), K("li")], [K("m2")])
                tt(V(19), V(19), V(20), ALU.subtract, [K("m1"), K("m2")], [K("m3")])
                tt(V(21), V(19), V(15), ALU.mult, [K("m3"), K("inv")], [K("c1i")])
                bc16 = lambda ap: ap.unsqueeze(2).to_broadcast([128, 32, 16])
                w1v = w1[:]
                w2v = w2[:]
                self.cmul("dve", bbr[:], bbi[:], bc16(V(18)), bc16(V(21)), br[:], bi[:], w1v, w2v,
                          [K("c1r"), K("c1i"), K("br"), K("bi")], K("bb"))
                for s_ in range(8):
                    k = 7 - s_
                    self.cmul("dve", E_re[:, :, s_, :], E_im[:, :, s_, :], bc16(P_re[:, k, :]), bc16(P_im[:, k, :]),
                              bbr[:], bbi[:], w1v, w2v, [K("P%dr" % k), K("P%di" % k), K("bbr"), K("bbi")], K("E%d" % s_))
                for t_ in range(8):
                    k = t_ + 1
                    self.cmul("dve", F_re[:, :, t_, :], F_im[:, :, t_, :], bc16(P_re[:, k, :]), bc16(P_im[:, k, :]),
                              cr[:], ci[:], w1v, w2v, [K("P%dr" % k), K("P%di" % k), K("cr"), K("ci")], K("F%d" % t_))
                allE = [K("E%d%s" % (i, c)) for i in range(8) for c in "ri"]
                allF = [K("F%d%s" % (i, c)) for i in range(8) for c in "ri"]
                fl = lambda t4: t4[:].rearrange("p g s h -> p g (s h)")
                kb.op("act", lambda e: e.copy(out=Eb_re[:], in_=fl(E_re)), r=allE, w=[K("Ebre")])
                kb.op("act", lambda e: e.copy(out=Eb_im[:], in_=fl(E_im)), r=allE, w=[K("Ebim")])
                kb.op("act", lambda e: e.copy(out=Vre[:], in_=fl(F_re)), r=allF, w=[K("Vre")])
                kb.op("act", lambda e: e.activation(out=Vim[:], in_=fl(F_im), func=AF.Copy, scale=-1.0), r=allF, w=[K("Vim")])
                tt(V(13), P_re[:, 8, :], P_re[:, 8, :], ALU.mult, [K("P8r"), K("inv")], [K("q1")])
                tt(V(14), P_im[:, 8, :], P_im[:, 8, :], ALU.mult, [K("P8i"), K("inv")], [K("q2")])
                tt(V(13), V(13), V(14), ALU.add, [K("q1"), K("q2")], [K("q3")])
                kb.op("dve", lambda e: e.reciprocal(out=V(15), in_=V(13)), r=[K("q3")], w=[K("i8")])
                tt(V(22), P_re[:, 8, :], V(15), ALU.mult, [K("P8r"), K("i8")], [K("nr")])
                tt(V(23), ALn[:, 0, :], V(15), ALU.mult, [K("ALn"), K("i8")], [K("ni")])
                w1g = w1[:].rearrange("p g h -> p (g h)").rearrange("p (a b) -> p a b", a=4)
                w2g = w2[:].rearrange("p g h -> p (g h)").rearrange("p (a b) -> p a b", a=4)
                for q4 in range(8):
                    g0, g1 = q4 * 4, q4 * 4 + 4
                    bcg = lambda ap: ap[:, g0:g1].unsqueeze(2).to_broadcast([128, 4, 128])
                    self.cmul("dve", Gb_re[:, g0:g1, :], Gb_im[:, g0:g1, :], bcg(V(22)), bcg(V(23)),
                              fl(E_re)[:, g0:g1, :], fl(E_im)[:, g0:g1, :], w1g, w2g,
                              [K("nr"), K("ni")] + allE, K("Gb%d" % q4))
                allG = [K("Gb%d%s" % (q4, c)) for q4 in range(8) for c in "ri"]
                cntp = 0
                for g in range(64):
                    gp, g2 = g // 2, g % 2
                    t = tp[cntp % 2]
                    tk = K("tp%d" % (cntp % 2))
                    cntp += 1
                    for ri, Eb, ek in ((0, Eb_re, K("Ebre")), (1, Eb_im, K("Ebim"))):
                        kb.op("pe", lambda e, t=t, ri=ri, Eb=Eb, gp=gp, g2=g2: e.transpose(
                            out=t[:, ri, 0:64], in_=Eb[64 * g2:64 * g2 + 64, gp, :],
                            identity=self.ident[64 * g2:64 * g2 + 64, 64 * g2:64 * g2 + 64]), r=[ek, "ident"], w=[tk])
                    kb.op("act", lambda e, t=t, g=g: e.copy(out=WinT[:, g, :, :], in_=t[:, 0:2, 0:64]), r=[tk], w=[K("WinT")])
                    pt = pT0[g % 2]
                    pk = K("pt0%d" % (g % 2))
                    kb.op("pe", lambda e, pt=pt, gp=gp, g2=g2: e.matmul(
                        pt[:], lhsT=Gb_re[64 * g2:64 * g2 + 64, gp, :], rhs=Vre[64 * g2:64 * g2 + 64, gp, :],
                        start=True, stop=False), r=allG + [K("Vre")], w=[pk])
                    kb.op("pe", lambda e, pt=pt, gp=gp, g2=g2: e.matmul(
                        pt[:], lhsT=Gb_im[64 * g2:64 * g2 + 64, gp, :], rhs=Vim[64 * g2:64 * g2 + 64, gp, :],
                        start=False, stop=True), r=allG + [K("Vim")], w=[pk])
                    kb.op("dve", lambda e, pt=pt: e.tensor_tensor(out=t0s[:], in0=pt[:], in1=maskt[:], op=ALU.mult),
                          r=[pk, K("maskt")], w=[K("t0s")])
                    kb.op("dve", lambda e, g=g: e.scalar_tensor_tensor(
                        out=T0[:, g, :], in0=identf[:], scalar=dcol[:, g:g + 1], in1=t0s[:], op0=ALU.mult, op1=ALU.add),
                        r=[K("identf"), K("dcol"), K("t0s")], w=[K("T0")])
                kb.barrier()
            hT = sb("hT8", [128, 8, 8, 512], BF16)
            with ExitStack() as es3:
                xt = [es3.enter_context(nc.sbuf_tensor(tag + "xt%d" % i, [128, 2, D], F32)) for i in range(2)]
                hbs = [es3.enter_context(nc.sbuf_tensor(tag + "hb%d" % i, [128, 2, D], BF16)) for i in range(2)]

                def load(b):
                    s = b % 2
                    kb.dma("sp", xt[s][:], src[b * TB:(b + 1) * TB, :].rearrange("(j p) d -> p j d", p=128),
                           r=[("xd", b)], w=[K("xt%d" % s)], sem=K("xt%d" % s))
                load(0)
                for b in range(NBt):
                    if b + 1 < NBt:
                        load(b + 1)
                    s = b % 2
                    hb = hbs[s]
                    for j in range(2):
                        self.prenorm_hb(xt[s][:, j, :], K("xt%d" % s), hb[:, j, :], K("hb%d" % s))
                    self.transposes(hb, K("hb%d" % s), 2, 8, hT, K("hT"), b * TB, tp, [K("tp0"), K("tp1")],
                                    lambda k0, n: self.gT[:, gi_pre, k0:k0 + n], "gT", layout8=True)
                for c_ in range(8):
                    kb.dma("sp", self.hd[c_, :, :, :], hT[:, c_, :, :], r=[K("hT")], w=["hd"], sem=K("hd%d" % (c_ % 2)))
                kb.barrier()
            es_tp.close()
            NR = 16
            NSL = 4
            Rg = [sb("Rg%d" % i, [128, 512], BF16) for i in range(NR)]
            X_ = [[[sb("X%d%d%d" % (sl, ab, ri), [128, 768], F32) for ri in range(2)] for ab in range(2)] for sl in range(NSL)]
            for sl_ in range(NSL):
                for ab_ in range(2):
                    for ri_ in range(2):
                        kb.op("dve", lambda e, t_=X_[sl_][ab_][ri_]: e.memset(t_[:, 0:256], 0.0), w=[K("Xpad")])
            Xh = [[sb("Xh%d%d" % (sl, ri), [128, 512], BF16) for ri in range(2)] for sl in range(NSL)]
            yg = [[sb("yg%d%d" % (sl, i), [128, 512], BF16) for i in range(2)] for sl in range(NSL)]
            xin = [[ps("xin%d%d" % (sl, i), [128, 512]) for i in range(2)] for sl in range(2)]
            yps = [[ps("yps%d%d" % (sl, i), [128, 512]) for i in range(2)] for sl in range(2)]

            def loadR(g):
                slot = g % NR
                c, g8 = g // 8, g % 8
                kb.dma("sp", Rg[slot][:, :], self.hd[c, 16 * g8:16 * g8 + 16, :, :].rearrange("h s b -> s h b"),
                       r=["hd"], w=[K("Rg%d" % slot)], sem=K("Rg%d" % slot))

            def pair_gen(gp, sl):
                psl = sl % 2
                Xk = lambda ab, ri: K("X%d%d%d" % (sl, ab, ri))
                for g2 in range(2):
                    g = 2 * gp + g2
                    R = Rg[g % NR]
                    rk = K("Rg%d" % (g % NR))
                    for ri in range(2):
                        kb.op("pe", lambda e, ri=ri, g=g, g2=g2, R=R: e.matmul(
                            xin[psl][ri][64 * g2:64 * g2 + 64, :], lhsT=WinT[:, g, ri, :], rhs=R[:, :],
                            start=True, stop=True), r=[K("WinT"), rk], w=[K("xin%d%d" % (psl, ri))])
                for ri in range(2):
                    kb.op("act", lambda e, ri=ri: e.copy(out=X_[sl][0][ri][:, 256:768], in_=xin[psl][ri][:]),
                          r=[K("xin%d%d" % (psl, ri)), K("Xpad")], w=[Xk(0, ri)])
                yield
                ab = 0
                for l in range(9):
                    d = 1 << l
                    cur, nxt = X_[sl][ab], X_[sl][1 - ab]
                    ar = ALr[:, l, gp:gp + 1]
                    ai = ALi[:, l, gp:gp + 1]
                    an = ALn[:, l, gp:gp + 1]
                    rk_ = [Xk(ab, 0), Xk(ab, 1)] + allAL
                    kb.op("dve", lambda e, cur=cur, nxt=nxt, d=d, ar=ar: e.scalar_tensor_tensor(
                        out=nxt[0][:, 256:768], in0=cur[0][:, 256 - d:768 - d], scalar=ar, in1=cur[0][:, 256:768],
                        op0=ALU.mult, op1=ALU.add), r=rk_, w=[Xk(1 - ab, 0)])
                    kb.op("dve", lambda e, cur=cur, nxt=nxt, d=d, ar=ar: e.scalar_tensor_tensor(
                        out=nxt[1][:, 256:768], in0=cur[1][:, 256 - d:768 - d], scalar=ar, in1=cur[1][:, 256:768],
                        op0=ALU.mult, op1=ALU.add), r=rk_, w=[Xk(1 - ab, 1)])
                    yield
                    kb.op("dve", lambda e, cur=cur, nxt=nxt, d=d, an=an: e.scalar_tensor_tensor(
                        out=nxt[0][:, 256:768], in0=cur[1][:, 256 - d:768 - d], scalar=an, in1=nxt[0][:, 256:768],
                        op0=ALU.mult, op1=ALU.add), r=rk_ + [Xk(1 - ab, 0)], w=[Xk(1 - ab, 0)])
                    kb.op("dve", lambda e, cur=cur, nxt=nxt, d=d, ai=ai: e.scalar_tensor_tensor(
                        out=nxt[1][:, 256:768], in0=cur[0][:, 256 - d:768 - d], scalar=ai, in1=nxt[1][:, 256:768],
                        op0=ALU.mult, op1=ALU.add), r=rk_ + [Xk(1 - ab, 1)], w=[Xk(1 - ab, 1)])
                    yield
                    ab = 1 - ab
                fin = X_[sl][ab]
                for ri in range(2):
                    kb.op("act", lambda e, ri=ri, fin=fin: e.copy(out=Xh[sl][ri][:], in_=fin[ri][:, 256:768]),
                          r=[Xk(ab, ri)], w=[K("Xh%d%d" % (sl, ri))])
                yield
                for g2 in range(2):
                    g = 2 * gp + g2
                    R = Rg[g % NR]
                    rk = K("Rg%d" % (g % NR))
                    yp = yps[psl][g2]
                    yk = K("yps%d%d" % (psl, g2))
                    kb.op("pe", lambda e, yp=yp, g=g, R=R: e.matmul(
                        yp[:, :], lhsT=T0[:, g, :], rhs=R[:, :], start=True, stop=False), r=[K("T0"), rk], w=[yk])
                    kb.op("pe", lambda e, yp=yp, g2=g2: e.matmul(
                        yp[:, 1:512], lhsT=Vre[64 * g2:64 * g2 + 64, gp, :], rhs=Xh[sl][0][64 * g2:64 * g2 + 64, 0:511],
                        start=False, stop=False), r=[K("Vre"), K("Xh%d0" % sl)], w=[yk])
                    kb.op("pe", lambda e, yp=yp, g2=g2: e.matmul(
                        yp[:, 1:512], lhsT=Vim[64 * g2:64 * g2 + 64, gp, :], rhs=Xh[sl][1][64 * g2:64 * g2 + 64, 0:511],
                        start=False, stop=True), r=[K("Vim"), K("Xh%d1" % sl)], w=[yk])
                    ygt = yg[sl][g2]
                    ygk = K("yg%d%d" % (sl, g2))
                    kb.op("act", lambda e, yp=yp, ygt=ygt: e.activation(out=ygt[:], in_=yp[:], func=AF.Gelu),
                          r=[yk], w=[ygk])
                    c, g8 = g // 8, g % 8
                    kb.dma("sp", self.yd[c, 16 * g8:16 * g8 + 16, :, :].rearrange("h t b -> t h b"), ygt[:, :],
                           r=[ygk], w=["yd"], sem=ygk)
                    yield

            for g in range(min(2 * NSL, 2 * NGP)):
                loadR(g)
            for gp0 in range(0, NGP, NSL):
                for g in range(2 * gp0 + 2 * NSL, min(2 * gp0 + 4 * NSL, 2 * NGP)):
                    loadR(g)
                gens = [pair_gen(gp0 + i, i) for i in range(NSL) if gp0 + i < NGP]
                while gens:
                    for gen in list(gens):
                        try:
                            next(gen)
                        except StopIteration:
                            gens.remove(gen)
            kb.barrier()

    def phase_glu(self, layer):
        kb = self.kb
        nc = self.nc
        js = layer // 2
        TB = 256
        NB = S // TB
        tag = "gl%d" % layer
        with ExitStack() as es:
            sb = lambda n, sh, dt: es.enter_context(nc.sbuf_tensor(tag + n, sh, dt))
            ps = lambda n, sh, dt=F32: es.enter_context(nc.psum_tensor(tag + n, sh, dt))
            K = lambda n: tag + n
            wgl = sb("wgl", [128, 8, 2 * D], BF16)
            bgl = sb("bgl", [1, 2 * D], BF16)
            ones = sb("ones", [1, 128], BF16)
            gpost = sb("gpost", [128, D], F32)
            xt = [sb("xt%d" % i, [128, 2, D], F32) for i in range(2)]
            sgm = sb("sgm", [128, D], F32)
            vt = sb("vt", [128, D], F32)
            tmp = sb("tmp", [128, D], F32)
            zp = [ps("zp%d" % i, [128, 4, 512]) for i in range(2)]
            ysb = [sb("ysb%d" % i, [128, 8, 8, 128], BF16) for i in range(2)]
            lc = [sb("lc%d" % i, [128, 8, 128], BF16) for i in range(2)]

            def loady(sbk):
                sl = sbk % 2
                for c_ in range(8):
                    kb.dma("sp", ysb[sl][:, c_, :, :], self.yd[c_, :, :, sbk * 128:(sbk + 1) * 128],
                           r=["yd"], w=[K("ysb%d" % sl)], sem=K("ysb%d" % sl))
            loady(0)
            self.load_w(wgl, K("wgl"), self.ssm_w_glu[js].rearrange("(kc p) n -> p kc n", p=128), 4, 2 * D, K("wgl"))
            kb.dma("pool", bgl[:], self.ssm_b_glu[js:js + 1, :], w=[K("bgl")], sem=K("bgl"))
            kb.op("dve", lambda e: e.memset(ones[:], 1.0), w=[K("ones")])
            kb.dma("sp", gpost[:], self.norm_g[layer, 3, :].partition_broadcast(128), w=["gpost"], sem=K("gp"))

            def load(b):
                s = b % 2
                kb.dma("sp", xt[s][:], self.xd[b * TB:(b + 1) * TB, :].rearrange("(j p) d -> p j d", p=128),
                       r=[("xd", b)], w=[K("xt%d" % s)], sem=K("xt%d" % s))
            def ctx(k):
                b, j = k // 2, k % 2
                return b, j, b % 2, ysb[(b // 4) % 2], K("ysb%d" % ((b // 4) % 2)), k % 8

            def copies(k):
                b, j, s, ys, ysk, tl = ctx(k)
                lcj, lck = lc[j], K("lc%d" % j)
                for kc in range(8):
                    if kc % 2 == 0:
                        kb.op("dve", lambda e, lcj=lcj, kc=kc, ys=ys, tl=tl: e.tensor_copy(
                            out=lcj[:, kc, :].rearrange("p (b t) -> p b t", b=16),
                            in_=ys[:, kc, :, 16 * tl:16 * tl + 16].rearrange("p t b -> p b t")), r=[ysk], w=[lck])
                    else:
                        kb.op("act", lambda e, lcj=lcj, kc=kc, ys=ys, tl=tl: e.copy(
                            out=lcj[:, kc, :].rearrange("p (b t) -> p b t", b=16),
                            in_=ys[:, kc, :, 16 * tl:16 * tl + 16].rearrange("p t b -> p b t")), r=[ysk], w=[lck])

            def mm(k):
                b, j, s, ys, ysk, tl = ctx(k)
                z, zk, lcj, lck = zp[j], K("zp%d" % j), lc[j], K("lc%d" % j)
                for ch in range(4):
                    for kc in range(8):
                        kb.op("pe", lambda e, z=z, ch=ch, kc=kc, lcj=lcj: e.matmul(
                            z[:, ch, :], lhsT=lcj[:, kc, :], rhs=wgl[:, kc, ch * 512:(ch + 1) * 512],
                            start=(kc == 0), stop=False), r=[lck, K("wgl")], w=[zk])
                    kb.op("pe", lambda e, z=z, ch=ch: e.matmul(
                        z[:, ch, :], lhsT=ones[:, :], rhs=bgl[:, ch * 512:(ch + 1) * 512], start=False, stop=True),
                        r=[K("ones"), K("bgl")], w=[zk])

            def post(k):
                b, j, s, ys, ysk, tl = ctx(k)
                z, zk = zp[j], K("zp%d" % j)
                kb.op("act", lambda e, z=z: e.activation(out=sgm[:], in_=z[:, 2:4, :].rearrange("p a b -> p (a b)"),
                                                         func=AF.Sigmoid), r=[zk], w=[K("sgm")])
                kb.op("dve", lambda e, z=z: e.tensor_tensor(out=vt[:], in0=z[:, 0:2, :].rearrange("p a b -> p (a b)"),
                                                            in1=sgm[:], op=ALU.mult), r=[zk, K("sgm")], w=[K("vt")])
                self.epilogue(vt[:], K("vt"), xt[s][:, j, :], K("xt%d" % s), gpost, 1.0, tmp)
                if j == 1:
                    kb.dma("sp", self.xd[b * TB:(b + 1) * TB, :].rearrange("(j p) d -> p j d", p=128), xt[s][:],
                           r=[K("xt%d" % s)], w=[("xd", b)], sem=K("xt%d" % s))

            NK = NB * 2
            load(0)
            copies(0)
            for k in range(NK):
                b, j = k // 2, k % 2
                if j == 0:
                    if b + 1 < NB:
                        load(b + 1)
                    if b % 4 == 0 and b // 4 + 1 < 4:
                        loady(b // 4 + 1)
                mm(k)
                if k + 1 < NK:
                    copies(k + 1)
                post(k)
            kb.barrier()


DEFAULT_PHASES = []
for _i in range(DEPTH):
    DEFAULT_PHASES.append(("ffn", _i, 0))
    DEFAULT_PHASES.append(("attn", _i) if _i % 2 == 0 else ("ssm", _i))
    DEFAULT_PHASES.append(("ffn", _i, 1))
    DEFAULT_PHASES.append(("ple", _i))
_cache = {}


def get_prog(phases):
    key = tuple(phases)
    if key not in _cache:
        pr = Prog(list(phases))
        pr.build()
        _cache[key] = pr
    return _cache[key]


def t5_onehot():
    n = np.arange(128)
    nf = np.maximum(n, 1).astype(np.float32)
    large = 16 + (np.log(nf / np.float32(16)) / np.float32(np.log(128 / 16)) * np.float32(16)).astype(np.int32)
    large = np.minimum(large, 31)
    bucket = np.where(n < 16, n, large)
    oh = np.zeros((32, 128), np.float32)
    oh[bucket, n] = 1.0
    oh[31, :] -= 1.0
    return oh


def host_consts():
    return {"ident_in": np.eye(128, dtype=np.float32),
            "jflip_in": np.ascontiguousarray(np.eye(128, dtype=np.float32)[::-1]),
            "oh_in": t5_onehot(),
            "mask_in": np.kron((np.arange(8)[None, :] >= np.arange(8)[:, None]).astype(np.float32),
                               np.ones((16, 16), np.float32))}


def kernel(phases=None, **inputs):
    phases = DEFAULT_PHASES if phases is None else phases
    pr = get_prog(phases)
    f = lambda a: np.ascontiguousarray(np.asarray(a, dtype=np.float32))
    x = f(inputs["x"])
    p = f(inputs["p"])
    shared = {
        "norm_g": f(inputs["norm_g"]),
        "ffn_w_in": f(inputs["ffn_w_in"]),
        "ffn_w_out": f(inputs["ffn_w_out"]),
        "ple_w_proj": f(inputs["ple_w_proj"]),
        "ple_w_gate": f(inputs["ple_w_gate"]),
    }
    for k in ("attn_w_qkv", "attn_w_o", "attn_lam", "attn_subln_g", "rel_bias", "ssm_lam_re", "ssm_lam_im",
              "ssm_log_dt", "ssm_b_re", "ssm_b_im", "ssm_c_re", "ssm_c_im", "ssm_d", "ssm_w_glu", "ssm_b_glu"):
        shared[k] = f(inputs[k])
    shared.update(host_consts())
    in_maps = []
    for c in range(NCORES):
        m = dict(shared)
        m["x"] = x[c]
        m["p"] = np.ascontiguousarray(p[:, c])
        in_maps.append(m)
    res = run_bass_kernel_spmd(pr.nc, in_maps, core_ids=list(range(NCORES)))
    return np.stack([res.results[c]["out"] for c in range(NCORES)], axis=0)
```
